# Optimizing a Trainium2 kernel written in Bass

```python
import jax, jax.numpy as jnp
from jax import lax
import numpy as np

D_MODEL = 2048
BATCH = 8
SEQ = 4096
DEPTH = 2

EPS = 1e-6
N_BRANCH = 4
D_FF = 4 * D_MODEL
POOL_DIM = 512
POOL_WINDOWS = (2, 4, 8, 16)
POOL_GROUPS = len(POOL_WINDOWS)
POOL_GDIM = POOL_DIM // POOL_GROUPS
CONV_DIM = 512
CONV_WIDTH = 31
SGU_DIM = 512
SGU_GROUPS = 4
SGU_GDIM = SGU_DIM // SGU_GROUPS
CHUNK = 128
MLA_HEADS = 8
Q_LORA = 512
KV_LORA = 512
QK_NOPE = 128
QK_ROPE = 64
V_DIM = 128
ROPE_THETA = 10000.0
ATTN_BLOCK = 128
OFF_POOL = 0
OFF_CONV = OFF_POOL + POOL_DIM
OFF_SGU = OFF_CONV + 2 * CONV_DIM
OFF_Q = OFF_SGU + 2 * SGU_DIM
OFF_KV = OFF_Q + Q_LORA
OFF_KR = OFF_KV + KV_LORA
OFF_GATE = OFF_KR + QK_ROPE
N_IN = OFF_GATE + N_BRANCH * D_MODEL

kernel_name = 'hybrid_gated_pool_conv_sgu_mla_block'


def rmsnorm(x, g):
    xf = x.astype(jnp.float32)
    y = xf * lax.rsqrt(jnp.mean(xf * xf, axis=-1, keepdims=True) + EPS)
    return (y * g.astype(jnp.float32)).astype(x.dtype)


def layernorm(x, g, b):
    xf = x.astype(jnp.float32)
    mu = jnp.mean(xf, axis=-1, keepdims=True)
    var = jnp.mean(jnp.square(xf - mu), axis=-1, keepdims=True)
    y = (xf - mu) * lax.rsqrt(var + EPS)
    return (y * g.astype(jnp.float32) + b.astype(jnp.float32)).astype(x.dtype)


def pool_mixer(a, pool_w, pool_scale):
    B, S, _ = a.shape
    af = a.astype(jnp.float32)
    csum = jnp.cumsum(af, axis=1)
    t = jnp.arange(S)
    means = []
    for gi, w in enumerate(POOL_WINDOWS):
        cs = csum[..., gi * POOL_GDIM:(gi + 1) * POOL_GDIM]
        lagged = jnp.pad(cs, ((0, 0), (w, 0), (0, 0)))[:, :S]
        count = jnp.minimum(t + 1, w).astype(jnp.float32)
        means.append((cs - lagged) / count[None, :, None])
    pooled = (jnp.concatenate(means, axis=-1) - af).astype(a.dtype)
    pooled = pooled.reshape(B, S, POOL_GROUPS, POOL_GDIM)
    mixed = jnp.einsum('bsgc,gcd->bsgd', pooled, pool_w).reshape(B, S, POOL_DIM)
    return mixed * pool_scale


def conformer_conv(c, conv_w, conv_b, norm_g, norm_b):
    a, gate = jnp.split(c, 2, axis=-1)
    glu = a * jax.nn.sigmoid(gate)
    padded = jnp.pad(glu, ((0, 0), (CONV_WIDTH - 1, 0), (0, 0)))
    y = lax.conv_general_dilated(
        padded, conv_w[:, None, :], window_strides=(1,), padding='VALID',
        dimension_numbers=('NWC', 'WIO', 'NWC'), feature_group_count=CONV_DIM)
    y = layernorm(y + conv_b, norm_g, norm_b)
    return jax.nn.silu(y)


def spatial_gating(z, norm_g, norm_b, w_s, b_s):
    z = jax.nn.gelu(z)
    u, v = jnp.split(z, 2, axis=-1)
    v = layernorm(v, norm_g, norm_b)
    B, S, _ = v.shape
    v = v.reshape(B, S // CHUNK, CHUNK, SGU_GROUPS, SGU_GDIM)
    mask = jnp.tril(jnp.ones((CHUNK, CHUNK), dtype=bool))
    w = jnp.where(mask[None], w_s, 0)
    sp = jnp.einsum('gts,bnsgc->bntgc', w, v) + b_s.T[None, None, :, :, None]
    return u * sp.reshape(B, S, SGU_DIM)


def apply_rope(x, cos, sin):
    x1, x2 = jnp.split(x.astype(jnp.float32), 2, axis=-1)
    return jnp.concatenate([x1 * cos - x2 * sin, x2 * cos + x1 * sin], axis=-1).astype(x.dtype)


def latent_attention(cq, ckv, kr, cos, sin, q_norm_g, w_uq, kv_norm_g, w_ukv, attn_proj):
    B, S, _ = cq.shape
    q = (rmsnorm(cq, q_norm_g) @ w_uq).reshape(B, S, MLA_HEADS, QK_NOPE + QK_ROPE)
    q_nope = q[..., :QK_NOPE]
    q_rope = apply_rope(q[..., QK_NOPE:], cos[:, :, None], sin[:, :, None])
    kv = (rmsnorm(ckv, kv_norm_g) @ w_ukv).reshape(B, S, MLA_HEADS, QK_NOPE + V_DIM)
    k_nope, v = kv[..., :QK_NOPE], kv[..., QK_NOPE:]
    k_rope = apply_rope(kr, cos, sin)
    scale = (QK_NOPE + QK_ROPE) ** -0.5
    outs = []
    for i in range(S // ATTN_BLOCK):
        q0, q1 = i * ATTN_BLOCK, (i + 1) * ATTN_BLOCK
        s = (jnp.einsum('bqhd,bkhd->bhqk', q_nope[:, q0:q1], k_nope[:, :q1])
             + jnp.einsum('bqhd,bkd->bhqk', q_rope[:, q0:q1], k_rope[:, :q1]))
        s = s.astype(jnp.float32) * scale
        mask = jnp.arange(q1)[None, :] <= jnp.arange(q0, q1)[:, None]
        s = jnp.where(mask, s, jnp.finfo(jnp.float32).min)
        p = jax.nn.softmax(s, axis=-1).astype(v.dtype)
        outs.append(jnp.einsum('bhqk,bkhd->bqhd', p, v[:, :q1]))
    o = jnp.concatenate(outs, axis=1).reshape(B, S, MLA_HEADS * V_DIM)
    return o @ attn_proj


def _normal(key, shape, scale):
    return jax.random.normal(key, shape, jnp.float32) * scale


def _gain(key, n):
    return 1.0 + 0.02 * jax.random.normal(key, (DEPTH, n), jnp.float32)


def _bias(key, n):
    return 0.02 * jax.random.normal(key, (DEPTH, n), jnp.float32)


def setup_inputs(seed: int = 0) -> dict:
    key = jax.random.key(seed)
    ks = jax.random.split(key, 27)
    L = DEPTH
    return {
        'x': _normal(ks[0], (BATCH, SEQ, D_MODEL), 1.0),
        'positions': jnp.broadcast_to(jnp.arange(SEQ, dtype=jnp.int32)[None, :], (BATCH, SEQ)),
        'pre_mix_g': _gain(ks[1], D_MODEL),
        'w_in': _normal(ks[2], (L, D_MODEL, N_IN), D_MODEL ** -0.5),
        'pool_w': _normal(ks[3], (L, POOL_GROUPS, POOL_GDIM, POOL_GDIM), POOL_GDIM ** -0.5),
        'pool_scale': _gain(ks[4], POOL_DIM),
        'pool_proj': _normal(ks[5], (L, POOL_DIM, D_MODEL), POOL_DIM ** -0.5),
        'conv_w': _normal(ks[6], (L, CONV_WIDTH, CONV_DIM), CONV_WIDTH ** -0.5),
        'conv_b': _bias(ks[7], CONV_DIM),
        'conv_norm_g': _gain(ks[8], CONV_DIM),
        'conv_norm_b': _bias(ks[9], CONV_DIM),
        'conv_proj': _normal(ks[10], (L, CONV_DIM, D_MODEL), CONV_DIM ** -0.5),
        'sgu_norm_g': _gain(ks[11], SGU_DIM),
        'sgu_norm_b': _bias(ks[12], SGU_DIM),
        'sgu_w': _normal(ks[13], (L, SGU_GROUPS, CHUNK, CHUNK), CHUNK ** -0.5),
        'sgu_b': 1.0 + 0.02 * jax.random.normal(ks[14], (L, SGU_GROUPS, CHUNK), jnp.float32),
        'sgu_proj': _normal(ks[15], (L, SGU_DIM, D_MODEL), SGU_DIM ** -0.5),
        'q_norm_g': _gain(ks[16], Q_LORA),
        'w_uq': _normal(ks[17], (L, Q_LORA, MLA_HEADS * (QK_NOPE + QK_ROPE)), Q_LORA ** -0.5),
        'kv_norm_g': _gain(ks[18], KV_LORA),
        'w_ukv': _normal(ks[19], (L, KV_LORA, MLA_HEADS * (QK_NOPE + V_DIM)), KV_LORA ** -0.5),
        'attn_proj': _normal(ks[20], (L, MLA_HEADS * V_DIM, D_MODEL), (MLA_HEADS * V_DIM) ** -0.5),
        'w_out': _normal(ks[21], (L, D_MODEL, D_MODEL), D_MODEL ** -0.5),
        'post_mix_g': _gain(ks[22], D_MODEL),
        'pre_mlp_g': _gain(ks[23], D_MODEL),
        'w_up': _normal(ks[24], (L, D_MODEL, D_FF), D_MODEL ** -0.5),
        'w_down': _normal(ks[25], (L, D_FF, D_MODEL), D_FF ** -0.5),
        'post_mlp_g': _gain(ks[26], D_MODEL),
    }


def reference(x, positions, pre_mix_g, w_in, pool_w, pool_scale, pool_proj, conv_w, conv_b,
              conv_norm_g, conv_norm_b, conv_proj, sgu_norm_g, sgu_norm_b, sgu_w, sgu_b,
              sgu_proj, q_norm_g, w_uq, kv_norm_g, w_ukv, attn_proj, w_out, post_mix_g,
              pre_mlp_g, w_up, w_down, post_mlp_g):
    B, S, _ = x.shape
    inv_freq = ROPE_THETA ** (-jnp.arange(0, QK_ROPE, 2, dtype=jnp.float32) / QK_ROPE)
    ang = positions.astype(jnp.float32)[..., None] * inv_freq
    cos, sin = jnp.cos(ang), jnp.sin(ang)
    for l in range(DEPTH):
        h = rmsnorm(x, pre_mix_g[l])
        z = h @ w_in[l]
        y_pool = pool_mixer(z[..., OFF_POOL:OFF_CONV], pool_w[l], pool_scale[l]) @ pool_proj[l]
        y_conv = conformer_conv(z[..., OFF_CONV:OFF_SGU], conv_w[l], conv_b[l],
                                conv_norm_g[l], conv_norm_b[l]) @ conv_proj[l]
        y_sgu = spatial_gating(z[..., OFF_SGU:OFF_Q], sgu_norm_g[l], sgu_norm_b[l],
                               sgu_w[l], sgu_b[l]) @ sgu_proj[l]
        y_attn = latent_attention(z[..., OFF_Q:OFF_KV], z[..., OFF_KV:OFF_KR],
                                  z[..., OFF_KR:OFF_GATE], cos, sin, q_norm_g[l], w_uq[l],
                                  kv_norm_g[l], w_ukv[l], attn_proj[l])
        gates = jax.nn.sigmoid(z[..., OFF_GATE:].reshape(B, S, N_BRANCH, D_MODEL))
        merged = (gates[:, :, 0] * y_pool + gates[:, :, 1] * y_conv
                  + gates[:, :, 2] * y_sgu + gates[:, :, 3] * y_attn)
        x = x + rmsnorm(merged @ w_out[l], post_mix_g[l])
        h = rmsnorm(x, pre_mlp_g[l])
        f = jnp.square(jax.nn.relu(h @ w_up[l])) @ w_down[l]
        x = x + rmsnorm(f, post_mlp_g[l])
    return x
```

```python
import contextlib
import numpy as np
import concourse.bass as bass
import concourse.mybir as mybir
from concourse.bass_utils import run_bass_kernel_spmd

F32 = mybir.dt.float32
BF16 = mybir.dt.bfloat16
I32 = mybir.dt.int32
AF = mybir.ActivationFunctionType
ALU = mybir.AluOpType
AX = mybir.AxisListType

D = 2048
DC = 16
TT = 512
NIN = 11840
OFF_POOL, OFF_CONV, OFF_SGU, OFF_Q, OFF_KV, OFF_KR, OFF_GATE = 0, 512, 1536, 2560, 3072, 3584, 3648
EPS = 1e-6
POOL_WINDOWS = (2, 4, 8, 16)
CONV_W = 31
HEADS = 8
SCALE = 192.0 ** -0.5
CW = 8192
TWO_PI = 6.283185307179586

OBJS = [("WIN_A", 16 * 128 * 16 * 128), ("WIN_V", 128 * 16 * 512), ("WIN_C", 10 * 128 * 16 * 128),
        ("WUQ", 16 * 128 * 4 * 128), ("WUKV_K", 8 * 128 * 4 * 128), ("WUKV_V", 128 * 4 * 1024),
        ("POOLW", 4 * 128 * 128), ("PROJ", 16 * 128 * 20 * 128), ("GATE", 16 * 128 * 64 * 128),
        ("WOUT", 16 * 128 * 16 * 128), ("WUP", 64 * 128 * 16 * 128), ("WDOWN", 16 * 128 * 64 * 128)]

VEC_SPEC = [("pre_mix_g", 16), ("post_mix_g", 16), ("pre_mlp_g", 16), ("post_mlp_g", 16),
            ("pool_scale", 4), ("conv_b", 4), ("conv_norm_g", 4), ("conv_norm_b", 4),
            ("q_norm_g", 4), ("kv_norm_g", 4), ("conv_w", 124)]
NVL = sum(n for _, n in VEC_SPEC)
VEC_OFF = {}
_o = 0
for _n, _c in VEC_SPEC:
    VEC_OFF[_n] = _o
    _o += _c
NGLOB = 4


def img_layout(L):
    off = {}
    for l in range(L):
        cur = 0
        for n, sz in OBJS:
            off[(l, n)] = (cur, sz)
            cur += sz
    rows = -(-cur // CW)
    rows = -(-rows // 8) * 8
    return off, rows


def _stat_chunks(W, colsets):
    colsets = np.asarray(colsets)
    nch = colsets.shape[0]
    KC = W.shape[0] // 128
    Wg = W[:, colsets.reshape(-1)].reshape(KC, 128, nch, 128)
    return np.ascontiguousarray(Wg.transpose(2, 1, 0, 3))


def _moving(W, cols):
    KC = W.shape[0] // 128
    return np.ascontiguousarray(W[:, cols].reshape(KC, 128, len(cols)).transpose(1, 0, 2))


def build_image(inp, L):
    off, rows = img_layout(L)
    img = np.zeros((L, rows * CW), np.float32)
    ar = np.arange

    def put(l, name, arr):
        o, sz = off[(l, name)]
        assert arr.size == sz, (name, arr.size, sz)
        img[l, o:o + sz] = arr.reshape(-1)

    for l in range(L):
        w_in = np.asarray(inp["w_in"][l])
        cols = [OFF_POOL + c * 128 + ar(128) for c in range(4)]
        for c in range(4):
            cols += [OFF_CONV + c * 128 + ar(128), OFF_CONV + 512 + c * 128 + ar(128)]
        cols += [OFF_SGU + c * 128 + ar(128) for c in range(4)]
        put(l, "WIN_A", _stat_chunks(w_in, cols))
        put(l, "WIN_V", _moving(w_in, OFF_SGU + 512 + ar(512)))
        kr = OFF_KR + ar(64)
        kr_sw = np.concatenate([kr[32:], kr[:32]])
        cols = [OFF_Q + c * 128 + ar(128) for c in range(4)]
        cols += [OFF_KV + c * 128 + ar(128) for c in range(4)]
        cols += [np.concatenate([kr, kr]), np.concatenate([kr_sw, kr_sw])]
        put(l, "WIN_C", _stat_chunks(w_in, cols))
        w_uq = np.asarray(inp["w_uq"][l])
        cols = [h * 192 + ar(128) for h in range(8)]
        for j in range(4):
            r0 = (2 * j) * 192 + 128 + ar(64)
            r1 = (2 * j + 1) * 192 + 128 + ar(64)
            cols.append(np.concatenate([r0, r1]))
        for j in range(4):
            r0 = (2 * j) * 192 + 128 + ar(64)
            r1 = (2 * j + 1) * 192 + 128 + ar(64)
            cols.append(np.concatenate([r0[32:], r0[:32], r1[32:], r1[:32]]))
        put(l, "WUQ", _stat_chunks(w_uq, cols))
        w_ukv = np.asarray(inp["w_ukv"][l])
        put(l, "WUKV_K", _stat_chunks(w_ukv, [h * 256 + ar(128) for h in range(8)]))
        put(l, "WUKV_V", _moving(w_ukv, np.concatenate([h * 256 + 128 + ar(128) for h in range(8)])))
        put(l, "POOLW", np.asarray(inp["pool_w"][l]))
        wcat = np.concatenate([np.asarray(inp["pool_proj"][l]), np.asarray(inp["conv_proj"][l]),
                               np.asarray(inp["sgu_proj"][l]), np.asarray(inp["attn_proj"][l])], axis=0)
        put(l, "PROJ", _stat_chunks(wcat, [m * 128 + ar(128) for m in range(16)]))
        cols = [OFF_GATE + b * 2048 + m * 128 + ar(128) for m in range(16) for b in range(4)]
        g = _stat_chunks(w_in, cols).reshape(16, 4, 128, 16, 128).transpose(0, 2, 1, 3, 4)
        put(l, "GATE", np.ascontiguousarray(g))
        put(l, "WOUT", _stat_chunks(np.asarray(inp["w_out"][l]), [m * 128 + ar(128) for m in range(16)]))
        put(l, "WUP", _stat_chunks(np.asarray(inp["w_up"][l]), [m * 128 + ar(128) for m in range(64)]))
        put(l, "WDOWN", _stat_chunks(np.asarray(inp["w_down"][l]), [m * 128 + ar(128) for m in range(16)]))
    return img.reshape(L, rows, CW)


def build_vecs(inp, L):
    v = np.zeros((128, L * NVL + NGLOB), np.float32)
    for l in range(L):
        for name, n in VEC_SPEC:
            a = np.asarray(inp[name][l], np.float32)
            if name == "conv_w":
                a = a.reshape(31 * 4, 128).T
            else:
                a = a.reshape(n, 128).T
            v[:, l * NVL + VEC_OFF[name]: l * NVL + VEC_OFF[name] + n] = a
    p = np.arange(128)
    inv_freq = (10000.0 ** (-(np.arange(0, 64, 2, dtype=np.float32)) / 64.0)).astype(np.float32)
    g0 = L * NVL
    v[:, g0 + 0] = inv_freq[p % 32]
    v[:, g0 + 1] = np.where((p % 64) < 32, 1.0, -1.0)
    v[:, g0 + 2] = -1.0
    v[:, g0 + 3] = 0.0
    return v


def build_consts():
    c = np.zeros((128, 3, 128), np.float32)
    c[:, 0, :] = np.eye(128, dtype=np.float32)
    q = np.arange(128)[:, None]
    k = np.arange(128)[None, :]
    c[:, 1, :] = np.where(k <= q, 0.0, -30000.0)
    c[:, 2, :] = np.where(q <= k, 1.0, 0.0)
    return c


class Reg:
    __slots__ = ("name", "w", "rd", "ps")

    def __init__(self, name, ps=False):
        self.name = name
        self.w = None
        self.rd = {}
        self.ps = ps


class Eng:
    def __init__(self, name, h, sem):
        self.name = name
        self.h = h
        self.sem = sem
        self.key = name
        self.cnt = 0
        self.waited = {}


class V:
    __slots__ = ("ap", "regs")

    def __init__(self, ap, *regs):
        self.ap = ap
        self.regs = regs


NDMA = 40


class Kb:
    def __init__(self, nc, stack):
        self.nc = nc
        sem = lambda n: stack.enter_context(nc.semaphore(n))
        self.pe = Eng("pe", nc.tensor, sem("s_pe"))
        self.act = Eng("act", nc.scalar, sem("s_act"))
        self.dve = Eng("dve", nc.vector, sem("s_dve"))
        self.pool = Eng("pool", nc.gpsimd, sem("s_pool"))
        self.sp = Eng("sp", nc.sync, sem("s_sp"))
        self.dsem = [sem(f"s_d{i}") for i in range(NDMA)]
        self.dexp = [0] * NDMA
        self.dpool = {"sp": list(range(0, 28)), "pool": list(range(28, 36)), "act": list(range(36, 40))}
        self.dpos = {"sp": 0, "pool": 0, "act": 0}
        self.out_toks = []

    def _wait(self, eng, tok):
        sem, key, val = tok
        if eng.waited.get(key, 0) >= val:
            return
        eng.h.wait_ge(sem, val)
        eng.waited[key] = val

    def _deps(self, eng, reads, writes, is_dma):
        def need(t):
            if t[1] == eng.key and not is_dma:
                return eng.name != "pe" and t[2] <= eng.cnt
            return True
        for r in reads:
            t = r.w
            if t is not None and need(t):
                self._wait(eng, t)
            if r.ps:
                for key, tok in r.rd.items():
                    if key != eng.key:
                        self._wait(eng, tok)
        for w in writes:
            t = w.w
            if t is not None and need(t):
                self._wait(eng, t)
            for key, tok in w.rd.items():
                if need(tok):
                    self._wait(eng, tok)

    def _mark(self, tok, reads, writes):
        for r in reads:
            r.rd[tok[1]] = tok
        for w in writes:
            w.w = tok
            w.rd = {}

    def op(self, eng, fn, reads, writes, signal=True):
        self._deps(eng, reads, writes, False)
        ins = fn(eng.h)
        if signal:
            ins.then_inc(eng.sem, 1)
            eng.cnt += 1
            tok = (eng.sem, eng.key, eng.cnt)
        else:
            tok = (eng.sem, eng.key, eng.cnt + 1)
        self._mark(tok, reads, writes)

    def dma(self, q, out, in_, is_out=False):
        reads = list(in_.regs)
        writes = list(out.regs)
        self._deps(q, reads, writes, True)
        lst = self.dpool[q.name]
        i = lst[self.dpos[q.name] % len(lst)]
        self.dpos[q.name] += 1
        sem = self.dsem[i]
        key = ("d", i)
        if self.dexp[i] > 0:
            self._wait(q, (sem, key, self.dexp[i]))
        self.dexp[i] += 16
        q.h.dma_start(out=out.ap, in_=in_.ap).then_inc(sem, 16)
        tok = (sem, key, self.dexp[i])
        self._mark(tok, reads, writes)
        if is_out:
            self.out_toks.append(tok)

    @staticmethod
    def _regs(*vs):
        out = []
        for v in vs:
            if isinstance(v, V):
                out.extend(v.regs)
        return out

    @staticmethod
    def _ap(v):
        return v.ap if isinstance(v, V) else v

    fence = None

    def _pe_fence(self):
        self.op(self.pe, self.fence, [], [], True)

    def mm(self, out, lhsT, rhs, start, stop, signal=True):
        fenced = signal and self.fence is not None
        self.op(self.pe, lambda h: h.matmul(out.ap, lhsT.ap, rhs.ap, start=start, stop=stop),
                self._regs(lhsT, rhs), self._regs(out), signal and not fenced)
        if fenced:
            self._pe_fence()

    def tr(self, out, in_, ident, signal=True):
        fenced = signal and self.fence is not None
        self.op(self.pe, lambda h: h.transpose(out.ap, in_.ap, ident.ap),
                self._regs(in_, ident), self._regs(out), signal and not fenced)
        if fenced:
            self._pe_fence()

    def actv(self, out, in_, func, bias=None, scale=None, accum=None):
        kw = {}
        if bias is not None:
            kw["bias"] = self._ap(bias)
        if scale is not None:
            kw["scale"] = self._ap(scale)
        if accum is not None:
            kw["accum_out"] = accum.ap
        self.op(self.act, lambda h: h.activation(out.ap, in_.ap, func, **kw),
                self._regs(in_, bias, scale), self._regs(out, accum))

    def tt(self, eng, out, a, b, op):
        self.op(eng, lambda h: h.tensor_tensor(out.ap, a.ap, b.ap, op), self._regs(a, b), self._regs(out))

    def ts(self, eng, out, a, s1, s2, op0, op1=None):
        if op1 is None:
            fn = lambda h: h.tensor_scalar(out.ap, a.ap, self._ap(s1), None, op0)
        else:
            fn = lambda h: h.tensor_scalar(out.ap, a.ap, self._ap(s1), self._ap(s2), op0, op1)
        self.op(eng, fn, self._regs(a, s1, s2), self._regs(out))

    def stt(self, out, a, s, b, op0, op1):
        self.op(self.dve, lambda h: h.scalar_tensor_tensor(out.ap, a.ap, self._ap(s), b.ap, op0, op1),
                self._regs(a, s, b), self._regs(out))

    def cp(self, eng, out, in_):
        if eng is self.act:
            self.op(eng, lambda h: h.copy(out.ap, in_.ap), self._regs(in_), self._regs(out))
        else:
            self.op(eng, lambda h: h.tensor_copy(out.ap, in_.ap), self._regs(in_), self._regs(out))

    def barrier(self):
        engs = [self.pe, self.act, self.dve, self.pool, self.sp]
        for e in engs:
            for f in engs:
                if f is not e and f.cnt > 0:
                    self._wait(e, (f.sem, f.key, f.cnt))
            for i in range(NDMA):
                if self.dexp[i] > 0:
                    self._wait(e, (self.dsem[i], ("d", i), self.dexp[i]))

    def finish(self):
        for tok in self.out_toks:
            self._wait(self.sp, tok)


class Tl:
    def __init__(self, k, stack, name, shape, dtype, nreg=1, psum=False):
        alloc = k.nc.psum_tensor if psum else k.nc.sbuf_tensor
        self.t = stack.enter_context(alloc("t_" + name, list(shape), dtype))
        self.r = [Reg(f"{name}.{i}", ps=psum) for i in range(nreg)]

    def v(self, idx=None, r=0):
        ap = self.t[:] if idx is None else self.t[idx]
        if r is None:
            return V(ap, *self.r)
        if isinstance(r, (list, tuple)):
            return V(ap, *[self.r[i] for i in r])
        return V(ap, self.r[r])


def build_program(S, L, dbg=(), stop_after=None):
    assert S % TT == 0
    NT = S // TT
    NB = S // 128
    nc = bass.Bass("TRN2", target_bir_lowering=False)
    off, rows = img_layout(L)
    NV = L * NVL + NGLOB
    sl = slice(None)

    def din(name, shape, dt=F32):
        return nc.dram_tensor(name, list(shape), dt, kind="ExternalInput").ap()

    x_in = din("x", [S, D])
    pos_in = din("pos", [128, S], I32)
    wsrc = [din(f"wsrc{l}", [rows, CW]) for l in range(L)]
    vecs_in = din("vecs", [128, NV])
    cst_in = din("cst", [128, 3, 128])
    sgb_in = din("sgu_gb", [L, 128, 2, 512])
    bsb_in = din("sgu_bsb", [L, 128, 4, 512])
    sgw_in = din("sgu_w", [L, 128, 4, 128])
    out_d = nc.dram_tensor("out", [S, D], F32, kind="ExternalOutput").ap()

    def scr(name, shape, dt):
        return nc.dram_tensor(name, list(shape), dt,
                              kind="ExternalOutput" if name in dbg else "Internal").ap()

    wall = [scr(f"wall{l}", [rows, CW], BF16) for l in range(L)]
    XT = scr("XT", [DC, 128, S], F32)
    HT = scr("HT", [DC, 128, S], BF16)
    BR = scr("BR", [12, 128, S], BF16)
    QN = scr("QN", [8, 128, S], BF16)
    QR = scr("QR", [4, 128, S], BF16)
    KN = scr("KN", [8, 128, S], BF16)
    KR = scr("KR", [128, S], BF16)
    VV = scr("VV", [NB, 128, 1024], BF16)
    OT = scr("OT", [8, 128, S], BF16)
    CS = scr("CS", [2, 128, S], F32)
    DBG = scr("DBG", [128, 2048], F32)
    DBG2 = scr("DBG2", [8, NB, 128, 132], F32)
    wflat = [w_.rearrange("r c -> (r c)") for w_ in wall]
    wsrc_flat = [w_.rearrange("r c -> (r c)") for w_ in wsrc]

    R_in = Reg("ext_in")
    R_XT = [Reg(f"XT{t}") for t in range(NT)]
    R_HT = [Reg(f"HT{t}") for t in range(NT)]
    R_BR = [Reg(f"BR{t}") for t in range(NT)]
    R_Q = [Reg(f"Q{t}") for t in range(NT)]
    R_K = [Reg(f"K{t}") for t in range(NT)]
    R_OT = [Reg(f"OT{t}") for t in range(8)]
    R_CS = Reg("CS")
    R_w = {}

    with contextlib.ExitStack() as top:
        k = Kb(nc, top)
        pe, act, dve, pool, sp = k.pe, k.act, k.dve, k.pool, k.sp
        vecs = Tl(k, top, "vecs", [128, NV], F32)
        cst = Tl(k, top, "cst", [128, 3, 128], F32)
        ident_b = Tl(k, top, "ident_b", [128, 128], BF16)
        ones_b = Tl(k, top, "ones_b", [128, 128], BF16)
        bnd = Tl(k, top, "bnd", [128, 4], F32)
        PS = [Tl(k, top, f"ps{i}", [128, 512], F32, psum=True) for i in range(6)]
        PB = [Tl(k, top, f"pb{i}", [128, 1024], BF16, psum=True) for i in range(2)]
        k.dma(sp, vecs.v(), V(vecs_in[:, :], R_in))
        k.dma(sp, cst.v(), V(cst_in[:, :, :], R_in))
        k.cp(dve, ident_b.v(), cst.v((sl, 0, sl)))
        k.op(dve, lambda h: h.memset(ones_b.t[:], 1.0), [], ones_b.r)
        _fap = PS[5].t[:, 0:1]
        k._deps(pe, ident_b.r, [], False)
        k.fence = lambda h_: h_.matmul(_fap, ident_b.t[:, :], ident_b.t[:, 0:1], start=True, stop=True)
        ident_f = cst.v((sl, 0, sl))
        maskneg = cst.v((sl, 1, sl))
        triu01 = cst.v((sl, 2, sl))
        rot = [0]

        def ps_next():
            i = rot[0]
            rot[0] = (i + 1) % 3
            return PS[i]

        def vcol(l, name, c):
            j = l * NVL + VEC_OFF[name] + c
            return V(vecs.t[:, j:j + 1], vecs.r[0])

        def gcol(j):
            jj = L * NVL + j
            return V(vecs.t[:, jj:jj + 1], vecs.r[0])

        for l in range(L):
            for name, _ in OBJS:
                o, sz = off[(l, name)]
                npc = max(1, sz // (4 * 1024 * 1024))
                pc = sz // npc
                assert pc * npc == sz and pc % 2048 == 0
                regs = []
                for i in range(npc):
                    a = o + i * pc
                    r = Reg(f"w{l}{name}{i}")
                    k.dma(pool, V(wflat[l][a:a + pc].rearrange("(r c) -> r c", c=2048), r),
                          V(wsrc_flat[l][a:a + pc].rearrange("(r c) -> r c", c=2048), R_in))
                    regs.append(r)
                R_w[(l, name)] = regs

        def wload(dst, l, name, blk0, nblk, per_blk, q=None):
            o, sz = off[(l, name)]
            a = o + blk0 * per_blk
            d = dst.ap
            if nblk == 1:
                src = wflat[l][a:a + per_blk].rearrange("(p x) -> p x", p=128)
                if d.ndim == 3:
                    d = d.rearrange("p a b -> p (a b)")
                elif d.ndim == 4:
                    d = d.rearrange("p a b c -> p (a b c)")
            else:
                src = wflat[l][a:a + nblk * per_blk].rearrange("(g p x) -> p g x", g=nblk, p=128)
                if d.ndim == 4:
                    d = d.rearrange("p g b c -> p g (b c)")
            k.dma(q or sp, V(d, *dst.regs), V(src, *R_w[(l, name)]))

        def rstd_from(ps, dim, rstd):
            k.ts(dve, rstd.v(), ps.v(), 1.0 / dim, EPS, ALU.mult, ALU.add)
            k.actv(rstd.v(), rstd.v(), AF.Sqrt)
            k.op(dve, lambda h: h.reciprocal(rstd.t[:], rstd.t[:]), rstd.r, rstd.r)

        def rmsnorm_fm(src, nch, gfun, dst, sq, rstd):
            ps = ps_next()
            for c in range(nch):
                s = sq[c % 2]
                k.actv(s.v(), src(c), AF.Square)
                k.mm(ps.v(), ones_b.v(), s.v(), c == 0, c == nch - 1)
            rstd_from(ps, nch * 128, rstd)
            for c in range(nch):
                k.stt(dst(c), src(c), gfun(c), rstd.v(), ALU.mult, ALU.mult)

        def gelu(out, src, tmp_a, tmp_b):
            k.actv(tmp_a, src, AF.Square)
            k.ts(dve, tmp_a, tmp_a, 0.044715, 1.0, ALU.mult, ALU.add)
            k.tt(dve, tmp_a, tmp_a, src, ALU.mult)
            k.actv(tmp_b, tmp_a, AF.Sigmoid, scale=1.5957691216057308)
            k.tt(dve, out, tmp_b, src, ALU.mult)

        with contextlib.ExitStack() as st:
            posi = Tl(k, st, "posi", [128, S], I32)
            ang = Tl(k, st, "ang", [128, S], F32)
            t1 = Tl(k, st, "rt1", [128, S], F32)
            t2 = Tl(k, st, "rt2", [128, S], F32)
            ni = Tl(k, st, "rni", [128, S], I32)
            res = Tl(k, st, "rres", [128, 2, S], F32)
            k.dma(sp, posi.v(), V(pos_in[:, :], R_in))
            k.cp(dve, ang.v(), posi.v())
            k.ts(dve, ang.v(), ang.v(), gcol(0), None, ALU.mult)
            for which in range(2):
                src = ang
                if which == 0:
                    k.ts(dve, t2.v(), ang.v(), float(np.pi / 2), None, ALU.add)
                    src = t2
                k.ts(dve, t1.v(), src.v(), float(1.0 / TWO_PI), None, ALU.mult)
                k.cp(dve, ni.v(), t1.v())
                k.cp(dve, t1.v(), ni.v())
                k.stt(t2.v(), t1.v(), -6.28125, src.v(), ALU.mult, ALU.add)
                k.stt(t2.v(), t1.v(), -(TWO_PI - 6.28125), t2.v(), ALU.mult, ALU.add)
                k.ts(dve, t1.v(), t2.v(), 0.0, TWO_PI, ALU.is_lt, ALU.mult)
                k.tt(dve, t2.v(), t2.v(), t1.v(), ALU.add)
                k.ts(dve, t1.v(), t2.v(), TWO_PI, -TWO_PI, ALU.is_ge, ALU.mult)
                k.tt(dve, t2.v(), t2.v(), t1.v(), ALU.add)
                k.ts(dve, t2.v(), t2.v(), -float(np.pi), -3.1415925, ALU.add, ALU.max)
                k.ts(dve, t2.v(), t2.v(), 3.1415925, None, ALU.min)
                k.actv(t1.v(), t2.v(), AF.Sin)
                sgn = gcol(2) if which == 0 else gcol(1)
                k.ts(dve, res.v((sl, which, sl)), t1.v(), sgn, None, ALU.mult)
            k.dma(sp, V(CS.rearrange("w p s -> p w s"), R_CS), res.v())
            k.barrier()

        def phase1(l):
            with contextlib.ExitStack() as st:
                T = lambda name, shape, dt, nreg=1: Tl(k, st, f"p1{l}_{name}", shape, dt, nreg)
                xT = T("xT", [128, 16, 512], F32)
                hT = T("hT", [128, 16, 512], BF16)
                wst = [T(f"w{i}", [128, 2, 16, 128], BF16) for i in range(2)]
                winv = T("winv", [128, 16, 512], BF16)
                poolw = T("poolw", [128, 4, 128], BF16)
                diag = [T(f"diag{i}", [128, 31, 128], BF16) for i in range(2)]
                wsT = T("wsT", [128, 4, 128], BF16)
                sgw = T("sgw", [128, 4, 128], F32)
                sgb = T("sgb", [128, 2, 512], F32)
                bsb = T("bsb", [128, 4, 512], F32)
                sq = [T(f"sq{i}", [128, 512], BF16) for i in range(2)]
                csq = [T(f"csq{i}", [128, 512], BF16) for i in range(2)]
                rstd = T("rstd", [128, 512], F32)
                xtok = T("xtok", [128, 2048], F32) if l == 0 else None
                pa = T("pa", [128, 4, 528], F32)
                pt = [T(f"pt{i}", [128, 528], F32) for i in range(2)]
                invc0 = T("invc0", [128, 4, 16], F32)
                pooled = T("pooled", [128, 4, 512], BF16)
                brout = T("brout", [128, 12, 512], BF16, 3)
                glu = T("glu", [128, 4, 544], BF16)
                sgm = [T(f"sgm{i}", [128, 512], F32) for i in range(2)]
                cy = T("cy", [128, 4, 512], F32)
                lnm = T("lnm", [128, 512], F32)
                lnr = T("lnr", [128, 512], F32)
                lnt = T("lnt", [128, 512], F32)
                ug = T("ug", [128, 4, 512], BF16)
                vg = T("vg", [128, 512], F32)
                vst = T("vst", [128, 8], F32)
                vnb = T("vnb", [128, 4, 512], BF16)
                ga = [T(f"ga{i}", [128, 512], F32) for i in range(2)]
                S1, S2 = PS[3], PS[4]

                wload(winv.v(), l, "WIN_V", 0, 1, 128 * 16 * 512)
                o, sz = off[(l, "POOLW")]
                k.dma(sp, poolw.v(), V(wflat[l][o:o + sz].rearrange("(g p d) -> p g d", g=4, p=128), *R_w[(l, "POOLW")]))
                k.dma(sp, sgb.v(), V(sgb_in[l], R_in))
                k.dma(sp, bsb.v(), V(bsb_in[l], R_in))
                k.dma(sp, sgw.v(), V(sgw_in[l], R_in))
                for g in range(4):
                    ps = ps_next()
                    k.tr(V(ps.t[:, 0:128], ps.r[0]), sgw.v((sl, g, sl)), ident_f)
                    k.tt(dve, wsT.v((sl, g, sl)), V(ps.t[:, 0:128], ps.r[0]), triu01, ALU.mult)
                k.op(pool, lambda h: h.memset(invc0.t[:], 0.0), [], invc0.r)
                for g, w in enumerate(POOL_WINDOWS):
                    for t in range(16):
                        val = 1.0 / min(t + 1, w)
                        k.op(pool, lambda h, g=g, t=t, val=val: h.memset(invc0.t[:, g, t:t + 1], val), [], invc0.r)

                for tt in range(NT):
                    t0 = tt * TT
                    if l == 0:
                        for tb in range(4):
                            k.dma(sp, xtok.v(), V(x_in[t0 + tb * 128:t0 + (tb + 1) * 128, :], R_in))
                            for cg in range(4):
                                ps = ps_next()
                                for j in range(4):
                                    c = cg * 4 + j
                                    k.tr(V(ps.t[:, j * 128:(j + 1) * 128], ps.r[0]),
                                         xtok.v((sl, slice(c * 128, (c + 1) * 128))), ident_f, signal=(j == 3))
                                k.cp(act if cg % 2 else dve,
                                     xT.v((sl, slice(cg * 4, cg * 4 + 4), slice(tb * 128, (tb + 1) * 128))),
                                     V(ps.t[:, :].rearrange("p (j t) -> p j t", j=4), ps.r[0]))
                        k.dma(sp, V(XT[:, :, t0:t0 + TT].rearrange("c p s -> p c s"), R_XT[tt]), xT.v())
                    else:
                        k.dma(sp, xT.v(), V(XT[:, :, t0:t0 + TT].rearrange("c p s -> p c s"), R_XT[tt]))
                    rmsnorm_fm(lambda c: xT.v((sl, c, sl)), 16, lambda c: vcol(l, "pre_mix_g", c),
                               lambda c: hT.v((sl, c, sl)), sq, rstd)
                    k.dma(sp, V(HT[:, :, t0:t0 + TT].rearrange("c p s -> p c s"), R_HT[tt]), hT.v())

                    def zchunk(wt, gi, ps):
                        for kc in range(16):
                            k.mm(ps.v(), wt.v((sl, gi, kc, sl)), hT.v((sl, kc, sl)), kc == 0, kc == 15,
                                 signal=(kc == 15))

                    if tt == 0:
                        k.op(pool, lambda h: h.memset(pa.t[:, :, 0:16], 0.0), [], pa.r)
                    else:
                        k.cp(pool, pa.v((sl, sl, slice(0, 16))), pa.v((sl, sl, slice(512, 528))))
                    for grp in range(2):
                        wt = wst[grp % 2]
                        wload(wt.v(), l, "WIN_A", grp * 2, 2, 128 * 16 * 128)
                        for gi in range(2):
                            ps = ps_next()
                            zchunk(wt, gi, ps)
                            k.cp(act, pa.v((sl, grp * 2 + gi, slice(16, 528))), ps.v())
                    for g, w in enumerate(POOL_WINDOWS):
                        cur = lambda lo, hi, g=g: pa.v((sl, g, slice(lo, hi)))
                        sh = 1
                        bi = 0
                        while sh < w:
                            dst = pt[bi % 2]
                            lo = 2 * sh - 1
                            k.tt(pool, V(dst.t[:, lo:528], dst.r[0]), cur(lo, 528), cur(lo - sh, 528 - sh), ALU.add)
                            cur = lambda lo, hi, dst=dst: V(dst.t[:, lo:hi], dst.r[0])
                            sh *= 2
                            bi += 1
                        k.stt(pooled.v((sl, g, sl)), cur(16, 528), 1.0 / w, pa.v((sl, g, slice(16, 528))),
                              ALU.mult, ALU.subtract)
                        if tt == 0:
                            k.tt(dve, V(lnt.t[:, 0:16], lnt.r[0]), cur(16, 32), invc0.v((sl, g, sl)), ALU.mult)
                            k.tt(dve, pooled.v((sl, g, slice(0, 16))), V(lnt.t[:, 0:16], lnt.r[0]),
                                 pa.v((sl, g, slice(16, 32))), ALU.subtract)
                    for g in range(4):
                        ps = ps_next()
                        k.mm(ps.v(), poolw.v((sl, g, sl)), pooled.v((sl, g, sl)), True, True)
                        k.actv(brout.v((sl, g, sl), 0), ps.v(), AF.Copy, scale=vcol(l, "pool_scale", g))

                    if tt == 0:
                        k.op(pool, lambda h: h.memset(glu.t[:, :, 0:32], 0.0), [], glu.r)
                    else:
                        k.cp(pool, glu.v((sl, sl, slice(0, 32))), glu.v((sl, sl, slice(512, 544))))
                    for c in range(4):
                        wt = wst[c % 2]
                        wload(wt.v(), l, "WIN_A", 4 + 2 * c, 2, 128 * 16 * 128)
                        psa = ps_next()
                        zchunk(wt, 0, psa)
                        psg = ps_next()
                        zchunk(wt, 1, psg)
                        sg_ = sgm[c % 2]
                        k.actv(sg_.v(), psg.v(), AF.Sigmoid)
                        k.tt(dve, glu.v((sl, c, slice(32, 544))), psa.v(), sg_.v(), ALU.mult)
                    for c in range(4):
                        dg = diag[c % 2]
                        for kk in range(31):
                            k.ts(pool if kk % 2 else dve, dg.v((sl, kk, sl)), ident_b.v(),
                                 vcol(l, "conv_w", kk * 4 + c), None, ALU.mult)
                        ps = ps_next()
                        for kk in range(31):
                            k.mm(ps.v(), dg.v((sl, kk, sl)), glu.v((sl, c, slice(kk + 2, kk + 2 + 512))),
                                 kk == 0, kk == 30, signal=(kk == 30))
                        k.actv(cy.v((sl, c, sl)), ps.v(), AF.Identity, bias=vcol(l, "conv_b", c))
                        s = sq[c % 2]
                        k.cp(dve, s.v(), cy.v((sl, c, sl)))
                        k.mm(S1.v(), ones_b.v(), s.v(), c == 0, c == 3)
                        s2 = csq[c % 2]
                        k.actv(s2.v(), cy.v((sl, c, sl)), AF.Square)
                        k.mm(S2.v(), ones_b.v(), s2.v(), c == 0, c == 3)
                    k.ts(dve, lnm.v(), S1.v(), 1.0 / 512, None, ALU.mult)
                    k.tt(dve, lnt.v(), lnm.v(), lnm.v(), ALU.mult)
                    k.stt(lnr.v(), S2.v(), 1.0 / 512, lnt.v(), ALU.mult, ALU.subtract)
                    k.ts(dve, lnr.v(), lnr.v(), EPS, None, ALU.add)
                    k.actv(lnr.v(), lnr.v(), AF.Sqrt)
                    k.op(dve, lambda h: h.reciprocal(lnr.t[:], lnr.t[:]), lnr.r, lnr.r)
                    for c in range(4):
                        k.tt(dve, lnt.v(), cy.v((sl, c, sl)), lnm.v(), ALU.subtract)
                        k.tt(pool, lnt.v(), lnt.v(), lnr.v(), ALU.mult)
                        k.actv(brout.v((sl, 4 + c, sl), 1), lnt.v(), AF.Silu,
                               scale=vcol(l, "conv_norm_g", c), bias=vcol(l, "conv_norm_b", c))

                    for grp in range(2):
                        wt = wst[grp % 2]
                        wload(wt.v(), l, "WIN_A", 12 + grp * 2, 2, 128 * 16 * 128)
                        for gi in range(2):
                            ps = ps_next()
                            zchunk(wt, gi, ps)
                            g = grp * 2 + gi
                            gelu(ug.v((sl, g, sl)), ps.v(), ga[0].v(), ga[1].v())
                    for tb in range(4):
                        ps = ps_next()
                        for kc in range(16):
                            k.mm(ps.v(), hT.v((sl, kc, slice(tb * 128, (tb + 1) * 128))), winv.v((sl, kc, sl)),
                                 kc == 0, kc == 15, signal=(kc == 15))
                        gelu(vg.v(), ps.v(), ga[0].v(), ga[1].v())
                        k.op(dve, lambda h: h.bn_stats(vst.t[:, 0:6], vg.t[:]), vg.r, vst.r)
                        k.op(dve, lambda h: h.bn_aggr(vst.t[:, 6:8], vst.t[:, 0:6]), vst.r, vst.r)
                        k.ts(dve, V(vst.t[:, 7:8], vst.r[0]), V(vst.t[:, 7:8], vst.r[0]), EPS, None, ALU.add)
                        k.actv(V(vst.t[:, 7:8], vst.r[0]), V(vst.t[:, 7:8], vst.r[0]), AF.Sqrt)
                        k.op(dve, lambda h: h.reciprocal(vst.t[:, 7:8], vst.t[:, 7:8]), vst.r, vst.r)
                        k.ts(dve, vg.v(), vg.v(), V(vst.t[:, 6:7], vst.r[0]), V(vst.t[:, 7:8], vst.r[0]),
                             ALU.subtract, ALU.mult)
                        k.tt(pool, vg.v(), vg.v(), sgb.v((sl, 0, sl)), ALU.mult)
                        k.tt(dve, vnb.v((sl, tb, sl)), vg.v(), sgb.v((sl, 1, sl)), ALU.add)
                    for g in range(4):
                        ps = ps_next()
                        for tb in range(4):
                            k.mm(V(ps.t[:, tb * 128:(tb + 1) * 128], ps.r[0]),
                                 vnb.v((sl, tb, slice(g * 128, (g + 1) * 128))), wsT.v((sl, g, sl)), True, True,
                                 signal=(tb == 3))
                        k.tt(dve, lnt.v(), ps.v(), bsb.v((sl, g, sl)), ALU.add)
                        k.tt(dve, brout.v((sl, 8 + g, sl), 2), lnt.v(), ug.v((sl, g, sl)), ALU.mult)
                    k.dma(sp, V(BR[:, :, t0:t0 + TT].rearrange("c p s -> p c s"), R_BR[tt]), brout.v(None, None))

        def phase1c(l):
            with contextlib.ExitStack() as st:
                T = lambda name, shape, dt, nreg=1: Tl(k, st, f"p1c{l}_{name}", shape, dt, nreg)
                hT = T("hT", [128, 16, 512], BF16)
                wst = [T(f"w{i}", [128, 2, 16, 128], BF16) for i in range(2)]
                wuq = T("wuq", [128, 16, 4, 128], BF16)
                wkk = T("wkk", [128, 8, 4, 128], BF16)
                wkv = T("wkv", [128, 4, 1024], BF16)
                cs = T("cs", [128, 2, 512], F32)
                cq = T("cq", [128, 4, 512], F32)
                cqn = T("cqn", [128, 4, 512], BF16)
                ckv = T("ckv", [128, 4, 512], F32)
                ckvn = T("ckvn", [128, 4, 512], BF16)
                sq = [T(f"sq{i}", [128, 512], BF16) for i in range(2)]
                rstd = T("rstd", [128, 512], F32)
                r1 = T("r1", [128, 512], F32)
                r2 = T("r2", [128, 512], F32)
                qn = T("qn", [128, 8, 512], BF16)
                qr = T("qr", [128, 4, 512], BF16)
                kn = T("kn", [128, 8, 512], BF16)
                krt = T("krt", [128, 512], BF16)
                vt = T("vt", [128, 4, 1024], BF16)
                sqn = T("sqn", [128, 8, 512], BF16)
                sqr = T("sqr", [128, 4, 512], BF16)
                mx = T("mx", [128, 2], F32)
                hones = [T(f"hones{i}", [128, 128], BF16) for i in range(2)]
                for i in range(2):
                    k.op(pool, lambda h_, i=i: h_.memset(hones[i].t[:], 0.0), [], hones[i].r)
                    k.op(pool, lambda h_, i=i: h_.memset(hones[i].t[64 * i:64 * i + 64, :], 1.0), [], hones[i].r)

                wload(wuq.v(), l, "WUQ", 0, 16, 128 * 4 * 128)
                wload(wkk.v(), l, "WUKV_K", 0, 8, 128 * 4 * 128)
                wload(wkv.v(), l, "WUKV_V", 0, 1, 128 * 4 * 1024)
                k.op(dve, lambda h: h.memset(bnd.t[:], 0.0), [], bnd.r)

                def rope(psA, psB, dst):
                    k.tt(dve, r1.v(), psA.v(), cs.v((sl, 0, sl)), ALU.mult)
                    k.tt(dve, r2.v(), psB.v(), cs.v((sl, 1, sl)), ALU.mult)
                    k.tt(pool, dst, r1.v(), r2.v(), ALU.add)

                def norm_max(col, parts):
                    ps = ps_next()
                    for i, (v, lt) in enumerate(parts):
                        k.mm(ps.v(), lt, v, i == 0, i == len(parts) - 1)
                    k.op(dve, lambda h: h.reduce_max(mx.t[:, 0:1], ps.t[:], AX.X), ps.r, mx.r)
                    k.tt(dve, V(bnd.t[:, col:col + 1], bnd.r[0]), V(bnd.t[:, col:col + 1], bnd.r[0]),
                         V(mx.t[:, 0:1], mx.r[0]), ALU.max)

                for tt in range(NT):
                    t0 = tt * TT
                    k.dma(sp, hT.v(), V(HT[:, :, t0:t0 + TT].rearrange("c p s -> p c s"), R_HT[tt]))
                    k.dma(sp, cs.v(), V(CS[:, :, t0:t0 + TT].rearrange("w p s -> p w s"), R_CS))

                    def zchunk(wt, gi, ps):
                        for kc in range(16):
                            k.mm(ps.v(), wt.v((sl, gi, kc, sl)), hT.v((sl, kc, sl)), kc == 0, kc == 15,
                                 signal=(kc == 15))

                    for grp in range(5):
                        wt = wst[grp % 2]
                        wload(wt.v(), l, "WIN_C", grp * 2, 2, 128 * 16 * 128)
                        if grp < 4:
                            dst = cq if grp < 2 else ckv
                            for gi in range(2):
                                ps = ps_next()
                                zchunk(wt, gi, ps)
                                k.cp(act, dst.v((sl, (grp % 2) * 2 + gi, sl)), ps.v())
                        else:
                            psA = ps_next()
                            zchunk(wt, 0, psA)
                            psB = ps_next()
                            zchunk(wt, 1, psB)
                            rope(psA, psB, krt.v())
                    rmsnorm_fm(lambda c: cq.v((sl, c, sl)), 4, lambda c: vcol(l, "q_norm_g", c),
                               lambda c: cqn.v((sl, c, sl)), sq, rstd)
                    rmsnorm_fm(lambda c: ckv.v((sl, c, sl)), 4, lambda c: vcol(l, "kv_norm_g", c),
                               lambda c: ckvn.v((sl, c, sl)), sq, rstd)

                    def small(wtile, ch, src, ps):
                        for kc in range(4):
                            k.mm(ps.v(), wtile.v((sl, ch, kc, sl)), src.v((sl, kc, sl)), kc == 0, kc == 3,
                                 signal=(kc == 3))

                    for h in range(8):
                        ps = ps_next()
                        small(wuq, h, cqn, ps)
                        k.cp(act, qn.v((sl, h, sl)), ps.v())
                    for j in range(4):
                        psA = ps_next()
                        small(wuq, 8 + j, cqn, psA)
                        psB = ps_next()
                        small(wuq, 12 + j, cqn, psB)
                        rope(psA, psB, qr.v((sl, j, sl)))
                    for h in range(8):
                        ps = ps_next()
                        small(wkk, h, ckvn, ps)
                        k.cp(act, kn.v((sl, h, sl)), ps.v())
                    for tb in range(4):
                        for half in range(2):
                            ps = ps_next()
                            for kc in range(4):
                                k.mm(ps.v(), ckvn.v((sl, kc, slice(tb * 128, (tb + 1) * 128))),
                                     wkv.v((sl, kc, slice(half * 512, (half + 1) * 512))), kc == 0, kc == 3,
                                     signal=(kc == 3))
                            k.cp(dve if half else act, vt.v((sl, tb, slice(half * 512, (half + 1) * 512))), ps.v())
                    k.actv(sqn.v(), qn.v(), AF.Square)
                    k.actv(sqr.v(), qr.v(), AF.Square)
                    for h in range(8):
                        norm_max(0, [(sqn.v((sl, h, sl)), ones_b.v()), (sqr.v((sl, h // 2, sl)), hones[h % 2].v())])
                    k.actv(sqn.v(), kn.v(), AF.Square)
                    k.actv(sq[0].v(), krt.v(), AF.Square)
                    for h in range(8):
                        norm_max(1, [(sqn.v((sl, h, sl)), ones_b.v()), (sq[0].v(), hones[0].v())])
                    k.dma(sp, V(QN[:, :, t0:t0 + TT].rearrange("c p s -> p c s"), R_Q[tt]), qn.v())
                    k.dma(sp, V(QR[:, :, t0:t0 + TT].rearrange("c p s -> p c s"), R_Q[tt]), qr.v())
                    k.dma(sp, V(KN[:, :, t0:t0 + TT].rearrange("c p s -> p c s"), R_K[tt]), kn.v())
                    k.dma(sp, V(KR[:, t0:t0 + TT], R_K[tt]), krt.v())
                    k.dma(sp, V(VV[tt * 4:(tt + 1) * 4, :, :].rearrange("b p c -> p b c"), R_K[tt]), vt.v())
                k.tt(dve, V(bnd.t[:, 3:4], bnd.r[0]), V(bnd.t[:, 0:1], bnd.r[0]), V(bnd.t[:, 1:2], bnd.r[0]), ALU.mult)
                k.actv(V(bnd.t[:, 3:4], bnd.r[0]), V(bnd.t[:, 3:4], bnd.r[0]), AF.Sqrt)
                k.ts(dve, V(bnd.t[:, 2:3], bnd.r[0]), V(bnd.t[:, 3:4], bnd.r[0]), -SCALE, None, ALU.mult)

        def phase2(l):
            with contextlib.ExitStack() as st:
                T = lambda name, shape, dt, nreg=1: Tl(k, st, f"p2{l}_{name}", shape, dt, nreg)
                KRz = [T(f"KRz{i}", [128, S], BF16) for i in range(2)]
                KTh = [T(f"KT{i}", [128, S], BF16) for i in range(2)]
                Vh = [T(f"V{i}", [128, NB, 128], BF16) for i in range(2)]
                QNh = [T(f"QN{i}", [128, S], BF16) for i in range(2)]
                QRp = [T(f"QR{i}", [128, S], BF16) for i in range(2)]
                OTh = [T(f"OT{i}", [128, S], BF16) for i in range(2)]
                pbuf = [T(f"p{i}", [128, 512], BF16) for i in range(3)]
                pT = [T(f"pT{i}", [128, 512], BF16) for i in range(3)]
                sd = [T(f"sd{i}", [128, 128], F32) for i in range(2)]
                rs = [T(f"rs{i}", [128, 16], F32) for i in range(2)]
                rsum = [T(f"rsum{i}", [128, 2], F32) for i in range(2)]
                ob = [T(f"ob{i}", [128, 128], BF16) for i in range(2)]
                negb = V(bnd.t[:, 2:3], bnd.r[0])
                allK = R_K
                dbg3 = [T(f"dbg3{i}", [128, 132], F32) for i in range(2)] if "DBG2" in dbg else None
                for i in range(2):
                    k.dma(sp, KRz[i].v(), V(KR[:, :], *allK))
                    z0 = 64 * (1 - i)
                    k.op(pool, lambda h_, i=i, z0=z0: h_.memset(KRz[i].t[z0:z0 + 64, :], 0.0), [], KRz[i].r)
                cnt = 0
                for h in range(8):
                    b = h % 2
                    r0 = (h % 2) * 64
                    k.dma(sp, KTh[b].v(), V(KN[h], *allK))
                    for v0 in range(0, NB, 8):
                        v1 = min(NB, v0 + 8)
                        k.dma(sp, Vh[b].v((sl, slice(v0, v1), sl)),
                              V(VV[v0:v1, :, h * 128:(h + 1) * 128].rearrange("b p c -> p b c"), *allK))
                    k.dma(sp, QNh[b].v(), V(QN[h], *R_Q))
                    k.dma(sp, QRp[b].v(), V(QR[h // 2], *R_Q))
                    for qb in range(NB):
                        q0 = qb * 128
                        nk = qb + 1
                        nch = (nk + 3) // 4
                        O = PS[3 + (qb % 2)]
                        rsb = rs[qb % 2]
                        k.op(pool, lambda h_, rsb=rsb: h_.memset(rsb.t[:], 0.0), [], rsb.r)
                        for j in range(nch):
                            kb0 = j * 4
                            nb = min(4, nk - kb0)
                            ncols = nb * 128
                            last = (j == nch - 1)
                            Sb = ps_next()
                            Sv = lambda lo, hi: V(Sb.t[:, lo:hi], Sb.r[0])
                            k.mm(Sv(0, ncols), QNh[b].v((sl, slice(q0, q0 + 128))),
                                 KTh[b].v((sl, slice(kb0 * 128, kb0 * 128 + ncols))), True, False, signal=False)
                            k.mm(Sv(0, ncols), QRp[b].v((sl, slice(q0, q0 + 128))),
                                 KRz[h % 2].v((sl, slice(kb0 * 128, kb0 * 128 + ncols))), False, True)
                            pb = pbuf[cnt % 3]
                            ptb = pT[cnt % 3]
                            PTp = PB[cnt % 2]
                            cnt += 1
                            nd = ncols - 128 if last else ncols
                            if last:
                                sdb = sd[qb % 2]
                                k.tt(dve, sdb.v(), Sv(nd, ncols), maskneg, ALU.add)
                            if nd > 0:
                                src = V(Sb.t[:, 0:nd], Sb.r[0], *(sd[qb % 2].r if last else []))
                                k.actv(V(pb.t[:, 0:nd], pb.r[0]), src, AF.Exp, bias=negb, scale=SCALE,
                                       accum=V(rsb.t[:, 2 * j:2 * j + 1], rsb.r[0]))
                            if last:
                                k.actv(V(pb.t[:, nd:ncols], pb.r[0]), sdb.v(), AF.Exp, bias=negb, scale=SCALE,
                                       accum=V(rsb.t[:, 2 * j + 1:2 * j + 2], rsb.r[0]))
                            for i in range(nb):
                                k.tr(V(PTp.t[:, i * 128:(i + 1) * 128], PTp.r[0]),
                                     V(pb.t[:, i * 128:(i + 1) * 128], pb.r[0]), ident_b.v(), signal=(i == nb - 1))
                            k.cp(dve, V(ptb.t[:, 0:ncols], ptb.r[0]), V(PTp.t[:, 0:ncols], PTp.r[0]))
                            if "DBG" in dbg and h == 0 and qb == 1:
                                dt_ = T("dbgt", [128, 1024], F32)
                                k.op(dve, lambda h_: h_.memset(dt_.t[:], 0.0), [], dt_.r)
                                k.cp(dve, V(dt_.t[:, 0:ncols], dt_.r[0]), V(pb.t[:, 0:ncols], pb.r[0]))
                                k.cp(dve, V(dt_.t[:, 512:512 + ncols], dt_.r[0]), V(ptb.t[:, 0:ncols], ptb.r[0]))
                                k.dma(sp, V(DBG[:, 0:1024], Reg("dbg")), dt_.v())
                            for i in range(nb):
                                k.mm(V(O.t[:, 0:128], O.r[0]), V(ptb.t[:, i * 128:(i + 1) * 128], ptb.r[0]),
                                     Vh[b].v((sl, kb0 + i, sl)), (j == 0 and i == 0), (last and i == nb - 1),
                                     signal=(i == nb - 1))
                        rsm = rsum[qb % 2]
                        k.op(dve, lambda h_, rsm=rsm, rsb=rsb: h_.reduce_sum(rsm.t[:, 0:1], rsb.t[:], AX.X),
                             rsb.r, rsm.r)
                        k.op(dve, lambda h_, rsm=rsm: h_.reciprocal(rsm.t[:, 1:2], rsm.t[:, 0:1]), rsm.r, rsm.r)
                        if "DBG" in dbg and h == 0 and qb == 1:
                            dt2 = T("dbgt2", [128, 256], F32)
                            k.op(dve, lambda h_: h_.memset(dt2.t[:], 0.0), [], dt2.r)
                            k.cp(dve, V(dt2.t[:, 0:16], dt2.r[0]), rsb.v())
                            k.cp(dve, V(dt2.t[:, 16:18], dt2.r[0]), rsm.v())
                            k.cp(dve, V(dt2.t[:, 128:256], dt2.r[0]), V(O.t[:, 0:128], O.r[0]))
                            k.dma(sp, V(DBG[:, 1024:1280], Reg("dbg2")), dt2.v())
                        if "DBG2" in dbg:
                            d3 = dbg3[qb % 2]
                            k.cp(dve, V(d3.t[:, 0:128], d3.r[0]), V(O.t[:, 0:128], O.r[0]))
                            k.cp(dve, V(d3.t[:, 128:130], d3.r[0]), rsm.v())
                            k.cp(dve, V(d3.t[:, 130:132], d3.r[0]), V(rsb.t[:, 0:2], rsb.r[0]))
                            k.dma(sp, V(DBG2[h, qb], Reg("dbg3")), d3.v())
                        obb = ob[qb % 2]
                        k.actv(obb.v(), V(O.t[:, 0:128], O.r[0]), AF.Copy, scale=V(rsm.t[:, 1:2], rsm.r[0]))
                        PTp = PB[cnt % 2]
                        cnt += 1
                        k.tr(V(PTp.t[:, 0:128], PTp.r[0]), obb.v(), ident_b.v())
                        k.cp(act, OTh[b].v((sl, slice(q0, q0 + 128))), V(PTp.t[:, 0:128], PTp.r[0]))
                    k.dma(sp, V(OT[h], R_OT[h]), OTh[b].v())

        def phase3a(l):
            with contextlib.ExitStack() as st:
                T = lambda name, shape, dt, nreg=1: Tl(k, st, f"p3a{l}_{name}", shape, dt, nreg)
                xT = T("xT", [128, 16, 512], F32)
                hT = T("hT", [128, 16, 512], BF16)
                br = T("br", [128, 12, 512], BF16)
                ot = T("ot", [128, 8, 512], BF16)
                merged = T("merged", [128, 16, 512], BF16, 16)
                fT = T("fT", [128, 16, 512], F32, 16)
                gw = [T(f"gw{i}", [128, 4, 16, 128], BF16) for i in range(2)]
                pw = [T(f"pw{i}", [128, 20, 128], BF16) for i in range(2)]
                sgm = [T(f"sgm{i}", [128, 512], F32) for i in range(2)]
                acc = T("acc", [128, 512], F32)
                tmp = [T(f"tmp{i}", [128, 512], F32) for i in range(2)]
                sq = [T(f"sq{i}", [128, 512], BF16) for i in range(2)]
                rstd = T("rstd", [128, 512], F32)
                SS = PS[3]
                kcs = [(0, 4, br, 0), (4, 4, br, 4), (8, 4, br, 8), (12, 8, ot, 0)]
                for tt in range(NT):
                    t0 = tt * TT
                    k.dma(sp, xT.v(), V(XT[:, :, t0:t0 + TT].rearrange("c p s -> p c s"), R_XT[tt]))
                    k.dma(sp, hT.v(), V(HT[:, :, t0:t0 + TT].rearrange("c p s -> p c s"), R_HT[tt]))
                    k.dma(sp, br.v(), V(BR[:, :, t0:t0 + TT].rearrange("c p s -> p c s"), R_BR[tt]))
                    k.dma(sp, ot.v(), V(OT[:, :, t0:t0 + TT].rearrange("c p s -> p c s"), *R_OT))
                    for m in range(16):
                        g_ = gw[m % 2]
                        p_ = pw[m % 2]
                        wload(g_.v(), l, "GATE", m, 1, 128 * 64 * 128)
                        wload(p_.v(), l, "PROJ", m, 1, 128 * 20 * 128)
                        for b in range(4):
                            G = ps_next()
                            for kc in range(16):
                                k.mm(G.v(), g_.v((sl, b, kc, sl)), hT.v((sl, kc, sl)), kc == 0, kc == 15,
                                     signal=(kc == 15))
                            Y = ps_next()
                            k0, nkc, src, s0 = kcs[b]
                            for kc in range(nkc):
                                k.mm(Y.v(), p_.v((sl, k0 + kc, sl)), src.v((sl, s0 + kc, sl)), kc == 0,
                                     kc == nkc - 1, signal=(kc == nkc - 1))
                            s_ = sgm[b % 2]
                            k.actv(s_.v(), G.v(), AF.Sigmoid)
                            if b == 0:
                                k.tt(dve, acc.v(), Y.v(), s_.v(), ALU.mult)
                            else:
                                t_ = tmp[b % 2]
                                k.tt(dve, t_.v(), Y.v(), s_.v(), ALU.mult)
                                if b < 3:
                                    k.tt(pool, acc.v(), acc.v(), t_.v(), ALU.add)
                                else:
                                    k.tt(pool, merged.v((sl, m, sl), m), acc.v(), t_.v(), ALU.add)
                    for mg in range(4):
                        wo = gw[mg % 2]
                        wload(wo.v(), l, "WOUT", mg * 4, 4, 128 * 16 * 128)
                        for mi in range(4):
                            m2 = mg * 4 + mi
                            Fb = ps_next()
                            for kc in range(16):
                                k.mm(Fb.v(), wo.v((sl, mi, kc, sl)), merged.v((sl, kc, sl), kc), kc == 0, kc == 15,
                                     signal=(kc == 15))
                            k.cp(dve, fT.v((sl, m2, sl), m2), Fb.v())
                            s = sq[m2 % 2]
                            k.actv(s.v(), fT.v((sl, m2, sl), m2), AF.Square)
                            k.mm(SS.v(), ones_b.v(), s.v(), m2 == 0, m2 == 15)
                    rstd_from(SS, 2048, rstd)
                    for c in range(16):
                        t_ = tmp[c % 2]
                        k.stt(t_.v(), fT.v((sl, c, sl), c), vcol(l, "post_mix_g", c), rstd.v(), ALU.mult, ALU.mult)
                        k.tt(pool, xT.v((sl, c, sl)), xT.v((sl, c, sl)), t_.v(), ALU.add)
                    k.dma(sp, V(XT[:, :, t0:t0 + TT].rearrange("c p s -> p c s"), R_XT[tt]), xT.v())

        def phase3b(l):
            with contextlib.ExitStack() as st:
                T = lambda name, shape, dt, nreg=1: Tl(k, st, f"p3b{l}_{name}", shape, dt, nreg)
                xT = T("xT", [128, 16, 512], F32)
                h2 = T("h2", [128, 16, 512], BF16)
                aT = T("aT", [128, 64, 512], BF16, 64)
                fT = T("fT", [128, 16, 512], F32, 16)
                wb = [T(f"w{i}", [128, 4, 16, 128], BF16) for i in range(2)]
                rr = [T(f"rr{i}", [128, 512], BF16) for i in range(2)]
                tmp = [T(f"tmp{i}", [128, 512], F32) for i in range(2)]
                sq = [T(f"sq{i}", [128, 512], BF16) for i in range(2)]
                rstd = T("rstd", [128, 512], F32)
                SS = PS[3]
                for tt in range(NT):
                    t0 = tt * TT
                    k.dma(sp, xT.v(), V(XT[:, :, t0:t0 + TT].rearrange("c p s -> p c s"), R_XT[tt]))
                    rmsnorm_fm(lambda c: xT.v((sl, c, sl)), 16, lambda c: vcol(l, "pre_mlp_g", c),
                               lambda c: h2.v((sl, c, sl)), sq, rstd)
                    for mg in range(16):
                        wu = wb[mg % 2]
                        wload(wu.v(), l, "WUP", mg * 4, 4, 128 * 16 * 128)
                        for mi in range(4):
                            m = mg * 4 + mi
                            ps = ps_next()
                            for kc in range(16):
                                k.mm(ps.v(), wu.v((sl, mi, kc, sl)), h2.v((sl, kc, sl)), kc == 0, kc == 15,
                                     signal=(kc == 15))
                            r_ = rr[m % 2]
                            k.actv(r_.v(), ps.v(), AF.Relu)
                            k.tt(pool if m % 2 else dve, aT.v((sl, m, sl), m), r_.v(), r_.v(), ALU.mult)
                    for m2 in range(16):
                        wd = wb[m2 % 2]
                        wload(wd.v(), l, "WDOWN", m2, 1, 128 * 64 * 128)
                        wdv = wd.t[:].rearrange("p a b c -> p (a b) c")
                        Fb = ps_next()
                        for kc in range(64):
                            k.mm(Fb.v(), V(wdv[:, kc, :], wd.r[0]), aT.v((sl, kc, sl), kc), kc == 0, kc == 63,
                                 signal=(kc == 63))
                        k.cp(dve, fT.v((sl, m2, sl), m2), Fb.v())
                        s = sq[m2 % 2]
                        k.actv(s.v(), fT.v((sl, m2, sl), m2), AF.Square)
                        k.mm(SS.v(), ones_b.v(), s.v(), m2 == 0, m2 == 15)
                    rstd_from(SS, 2048, rstd)
                    for c in range(16):
                        t_ = tmp[c % 2]
                        k.stt(t_.v(), fT.v((sl, c, sl), c), vcol(l, "post_mlp_g", c), rstd.v(), ALU.mult, ALU.mult)
                        k.tt(pool, xT.v((sl, c, sl)), xT.v((sl, c, sl)), t_.v(), ALU.add)
                    if l < L - 1:
                        k.dma(sp, V(XT[:, :, t0:t0 + TT].rearrange("c p s -> p c s"), R_XT[tt]), xT.v())
                    else:
                        for tb in range(4):
                            c0 = 4 * (tb % 2)
                            regs = list(range(c0, c0 + 4))
                            stg = fT.t[:, c0:c0 + 4, :].rearrange("p a b -> p (a b)")
                            for cg in range(4):
                                ps = ps_next()
                                for j in range(4):
                                    c = cg * 4 + j
                                    k.tr(V(ps.t[:, j * 128:(j + 1) * 128], ps.r[0]),
                                         xT.v((sl, c, slice(tb * 128, (tb + 1) * 128))), ident_f, signal=(j == 3))
                                k.cp(act if cg % 2 else dve,
                                     V(stg[:, cg * 512:(cg + 1) * 512], *[fT.r[i] for i in regs]), ps.v())
                            k.dma(sp, V(out_d[t0 + tb * 128:t0 + (tb + 1) * 128, :], Reg("o")),
                                  V(stg, *[fT.r[i] for i in regs]), is_out=True)

        phases = []
        for l in range(L):
            phases += [("p1", phase1, l), ("p1c", phase1c, l), ("p2", phase2, l), ("p3a", phase3a, l),
                       ("p3b", phase3b, l)]
        for name, fn, l in phases:
            fn(l)
            k.barrier()
            if stop_after == (name, l):
                break
        if stop_after is not None:
            for i in range(NDMA):
                if k.dexp[i] > 0:
                    k._wait(sp, (k.dsem[i], ("d", i), k.dexp[i]))
        k.finish()
    return nc


_CACHE = {}


def make_inputs(inp, S, L, n_cores):
    img = build_image(inp, L)
    vecs = build_vecs(inp, L)
    cst = build_consts()
    sgb = np.zeros((L, 128, 2, 512), np.float32)
    bsb = np.zeros((L, 128, 4, 512), np.float32)
    sgw = np.zeros((L, 128, 4, 128), np.float32)
    for l in range(L):
        sgb[l, :, 0, :] = np.asarray(inp["sgu_norm_g"][l])[None, :]
        sgb[l, :, 1, :] = np.asarray(inp["sgu_norm_b"][l])[None, :]
        b = np.asarray(inp["sgu_b"][l])
        bsb[l] = np.tile(b[None, :, :], (128, 1, 4))
        sgw[l] = np.asarray(inp["sgu_w"][l]).transpose(1, 0, 2)
    x = np.asarray(inp["x"], np.float32)
    pos = np.asarray(inp["positions"]).astype(np.int32)
    maps = []
    for c in range(n_cores):
        maps.append({
            "x": np.ascontiguousarray(x[c, :S]),
            "pos": np.ascontiguousarray(np.broadcast_to(pos[c, :S][None, :], (128, S))),
            **{f"wsrc{l}": img[l] for l in range(L)}, "vecs": vecs, "cst": cst, "sgu_gb": sgb, "sgu_bsb": bsb, "sgu_w": sgw,
        })
    return maps


def kernel(**inputs):
    S, L, n = 4096, 2, 8
    key = (S, L)
    if key not in _CACHE:
        _CACHE[key] = build_program(S, L)
    nc = _CACHE[key]
    maps = make_inputs(inputs, S, L, n)
    res = run_bass_kernel_spmd(nc, maps, core_ids=list(range(n)))
    out = np.stack([np.asarray(res.results[c]["out"], np.float32) for c in range(n)], axis=0)
    return out
```

```python
import contextlib
import numpy as np
import concourse.bass as bass
import concourse.mybir as mybir
from concourse.bass_utils import run_bass_kernel_spmd

F32 = mybir.dt.float32
BF16 = mybir.dt.bfloat16
I32 = mybir.dt.int32
AF = mybir.ActivationFunctionType
ALU = mybir.AluOpType
AX = mybir.AxisListType

D = 2048
DC = 16
TT = 512
NIN = 11840
OFF_POOL, OFF_CONV, OFF_SGU, OFF_Q, OFF_KV, OFF_KR, OFF_GATE = 0, 512, 1536, 2560, 3072, 3584, 3648
EPS = 1e-6
POOL_WINDOWS = (2, 4, 8, 16)
CONV_W = 31
HEADS = 8
SCALE = 192.0 ** -0.5
CW = 8192
TWO_PI = 6.283185307179586

OBJS = [("WIN_A", 16 * 128 * 16 * 128), ("WIN_V", 128 * 16 * 512), ("WIN_C", 10 * 128 * 16 * 128),
        ("WUQ", 16 * 128 * 4 * 128), ("WUKV_K", 8 * 128 * 4 * 128), ("WUKV_V", 128 * 4 * 1024),
        ("POOLW", 4 * 128 * 128), ("PROJ", 16 * 128 * 20 * 128), ("GATE", 16 * 128 * 64 * 128),
        ("WOUT", 16 * 128 * 16 * 128), ("WUP", 64 * 128 * 16 * 128), ("WDOWN", 16 * 128 * 64 * 128)]

VEC_SPEC = [("pre_mix_g", 16), ("post_mix_g", 16), ("pre_mlp_g", 16), ("post_mlp_g", 16),
            ("pool_scale", 4), ("conv_b", 4), ("conv_norm_g", 4), ("conv_norm_b", 4),
            ("q_norm_g", 4), ("kv_norm_g", 4), ("conv_w", 124)]
NVL = sum(n for _, n in VEC_SPEC)
VEC_OFF = {}
_o = 0
for _n, _c in VEC_SPEC:
    VEC_OFF[_n] = _o
    _o += _c
NGLOB = 4


def img_layout(L):
    off = {}
    for l in range(L):
        cur = 0
        for n, sz in OBJS:
            off[(l, n)] = (cur, sz)
            cur += sz
    rows = -(-cur // CW)
    rows = -(-rows // 8) * 8
    return off, rows


def _stat_chunks(W, colsets):
    colsets = np.asarray(colsets)
    nch = colsets.shape[0]
    KC = W.shape[0] // 128
    Wg = W[:, colsets.reshape(-1)].reshape(KC, 128, nch, 128)
    return np.ascontiguousarray(Wg.transpose(2, 1, 0, 3))


def _moving(W, cols):
    KC = W.shape[0] // 128
    return np.ascontiguousarray(W[:, cols].reshape(KC, 128, len(cols)).transpose(1, 0, 2))


def build_image(inp, L):
    off, rows = img_layout(L)
    img = np.zeros((L, rows * CW), np.float32)
    ar = np.arange

    def put(l, name, arr):
        o, sz = off[(l, name)]
        assert arr.size == sz, (name, arr.size, sz)
        img[l, o:o + sz] = arr.reshape(-1)

    for l in range(L):
        w_in = np.asarray(inp["w_in"][l])
        cols = [OFF_POOL + c * 128 + ar(128) for c in range(4)]
        for c in range(4):
            cols += [OFF_CONV + c * 128 + ar(128), OFF_CONV + 512 + c * 128 + ar(128)]
        cols += [OFF_SGU + c * 128 + ar(128) for c in range(4)]
        put(l, "WIN_A", _stat_chunks(w_in, cols))
        put(l, "WIN_V", _moving(w_in, OFF_SGU + 512 + ar(512)))
        kr = OFF_KR + ar(64)
        kr_sw = np.concatenate([kr[32:], kr[:32]])
        cols = [OFF_Q + c * 128 + ar(128) for c in range(4)]
        cols += [OFF_KV + c * 128 + ar(128) for c in range(4)]
        cols += [np.concatenate([kr, kr]), np.concatenate([kr_sw, kr_sw])]
        put(l, "WIN_C", _stat_chunks(w_in, cols))
        w_uq = np.asarray(inp["w_uq"][l])
        cols = [h * 192 + ar(128) for h in range(8)]
        for j in range(4):
            r0 = (2 * j) * 192 + 128 + ar(64)
            r1 = (2 * j + 1) * 192 + 128 + ar(64)
            cols.append(np.concatenate([r0, r1]))
        for j in range(4):
            r0 = (2 * j) * 192 + 128 + ar(64)
            r1 = (2 * j + 1) * 192 + 128 + ar(64)
            cols.append(np.concatenate([r0[32:], r0[:32], r1[32:], r1[:32]]))
        put(l, "WUQ", _stat_chunks(w_uq, cols))
        w_ukv = np.asarray(inp["w_ukv"][l])
        put(l, "WUKV_K", _stat_chunks(w_ukv, [h * 256 + ar(128) for h in range(8)]))
        put(l, "WUKV_V", _moving(w_ukv, np.concatenate([h * 256 + 128 + ar(128) for h in range(8)])))
        put(l, "POOLW", np.asarray(inp["pool_w"][l]))
        wcat = np.concatenate([np.asarray(inp["pool_proj"][l]), np.asarray(inp["conv_proj"][l]),
                               np.asarray(inp["sgu_proj"][l]), np.asarray(inp["attn_proj"][l])], axis=0)
        put(l, "PROJ", _stat_chunks(wcat, [m * 128 + ar(128) for m in range(16)]))
        cols = [OFF_GATE + b * 2048 + m * 128 + ar(128) for m in range(16) for b in range(4)]
        g = _stat_chunks(w_in, cols).reshape(16, 4, 128, 16, 128).transpose(0, 2, 1, 3, 4)
        put(l, "GATE", np.ascontiguousarray(g))
        put(l, "WOUT", _stat_chunks(np.asarray(inp["w_out"][l]), [m * 128 + ar(128) for m in range(16)]))
        put(l, "WUP", _stat_chunks(np.asarray(inp["w_up"][l]), [m * 128 + ar(128) for m in range(64)]))
        put(l, "WDOWN", _stat_chunks(np.asarray(inp["w_down"][l]), [m * 128 + ar(128) for m in range(16)]))
    return img.reshape(L, rows, CW)


def build_vecs(inp, L):
    v = np.zeros((128, L * NVL + NGLOB), np.float32)
    for l in range(L):
        for name, n in VEC_SPEC:
            a = np.asarray(inp[name][l], np.float32)
            if name == "conv_w":
                a = a.reshape(31 * 4, 128).T
            else:
                a = a.reshape(n, 128).T
            v[:, l * NVL + VEC_OFF[name]: l * NVL + VEC_OFF[name] + n] = a
    p = np.arange(128)
    inv_freq = (10000.0 ** (-(np.arange(0, 64, 2, dtype=np.float32)) / 64.0)).astype(np.float32)
    g0 = L * NVL
    v[:, g0 + 0] = inv_freq[p % 32]
    v[:, g0 + 1] = np.where((p % 64) < 32, 1.0, -1.0)
    v[:, g0 + 2] = -1.0
    v[:, g0 + 3] = 0.0
    return v


def build_consts():
    c = np.zeros((128, 3, 128), np.float32)
    c[:, 0, :] = np.eye(128, dtype=np.float32)
    q = np.arange(128)[:, None]
    k = np.arange(128)[None, :]
    c[:, 1, :] = np.where(k <= q, 0.0, -30000.0)
    c[:, 2, :] = np.where(q <= k, 1.0, 0.0)
    return c


class Reg:
    __slots__ = ("name", "w", "rd", "ps")

    def __init__(self, name, ps=False):
        self.name = name
        self.w = None
        self.rd = {}
        self.ps = ps


class Eng:
    def __init__(self, name, h, sem):
        self.name = name
        self.h = h
        self.sem = sem
        self.key = name
        self.cnt = 0
        self.waited = {}


class V:
    __slots__ = ("ap", "regs")

    def __init__(self, ap, *regs):
        self.ap = ap
        self.regs = regs


NDMA = 40


class Kb:
    def __init__(self, nc, stack):
        self.nc = nc
        sem = lambda n: stack.enter_context(nc.semaphore(n))
        self.pe = Eng("pe", nc.tensor, sem("s_pe"))
        self.act = Eng("act", nc.scalar, sem("s_act"))
        self.dve = Eng("dve", nc.vector, sem("s_dve"))
        self.pool = Eng("pool", nc.gpsimd, sem("s_pool"))
        self.sp = Eng("sp", nc.sync, sem("s_sp"))
        self.dsem = [sem(f"s_d{i}") for i in range(NDMA)]
        self.dexp = [0] * NDMA
        self.dpool = {"sp": list(range(0, 28)), "pool": list(range(28, 36)), "act": list(range(36, 40))}
        self.dpos = {"sp": 0, "pool": 0, "act": 0}
        self.out_toks = []

    def _wait(self, eng, tok):
        sem, key, val = tok
        if eng.waited.get(key, 0) >= val:
            return
        eng.h.wait_ge(sem, val)
        eng.waited[key] = val

    def _deps(self, eng, reads, writes, is_dma):
        def need(t):
            if t[1] == eng.key and not is_dma:
                return eng.name != "pe" and t[2] <= eng.cnt
            return True
        for r in reads:
            t = r.w
            if t is not None and need(t):
                self._wait(eng, t)
            if r.ps:
                for key, tok in r.rd.items():
                    if key != eng.key:
                        self._wait(eng, tok)
        for w in writes:
            t = w.w
            if t is not None and need(t):
                self._wait(eng, t)
            for key, tok in w.rd.items():
                if need(tok):
                    self._wait(eng, tok)

    def _mark(self, tok, reads, writes):
        for r in reads:
            r.rd[tok[1]] = tok
        for w in writes:
            w.w = tok
            w.rd = {}

    def op(self, eng, fn, reads, writes, signal=True):
        self._deps(eng, reads, writes, False)
        ins = fn(eng.h)
        if signal:
            ins.then_inc(eng.sem, 1)
            eng.cnt += 1
            tok = (eng.sem, eng.key, eng.cnt)
        else:
            tok = (eng.sem, eng.key, eng.cnt + 1)
        self._mark(tok, reads, writes)

    def dma(self, q, out, in_, is_out=False):
        reads = list(in_.regs)
        writes = list(out.regs)
        self._deps(q, reads, writes, True)
        lst = self.dpool[q.name]
        i = lst[self.dpos[q.name] % len(lst)]
        self.dpos[q.name] += 1
        sem = self.dsem[i]
        key = ("d", i)
        if self.dexp[i] > 0:
            self._wait(q, (sem, key, self.dexp[i]))
        self.dexp[i] += 16
        q.h.dma_start(out=out.ap, in_=in_.ap).then_inc(sem, 16)
        tok = (sem, key, self.dexp[i])
        self._mark(tok, reads, writes)
        if is_out:
            self.out_toks.append(tok)

    @staticmethod
    def _regs(*vs):
        out = []
        for v in vs:
            if isinstance(v, V):
                out.extend(v.regs)
        return out

    @staticmethod
    def _ap(v):
        return v.ap if isinstance(v, V) else v

    fence = None

    def _pe_fence(self):
        self.op(self.pe, self.fence, [], [], True)

    def mm(self, out, lhsT, rhs, start, stop, signal=True):
        fenced = signal and self.fence is not None
        self.op(self.pe, lambda h: h.matmul(out.ap, lhsT.ap, rhs.ap, start=start, stop=stop),
                self._regs(lhsT, rhs), self._regs(out), signal and not fenced)
        if fenced:
            self._pe_fence()

    def tr(self, out, in_, ident, signal=True):
        fenced = signal and self.fence is not None
        self.op(self.pe, lambda h: h.transpose(out.ap, in_.ap, ident.ap),
                self._regs(in_, ident), self._regs(out), signal and not fenced)
        if fenced:
            self._pe_fence()

    def actv(self, out, in_, func, bias=None, scale=None, accum=None):
        kw = {}
        if bias is not None:
            kw["bias"] = self._ap(bias)
        if scale is not None:
            kw["scale"] = self._ap(scale)
        if accum is not None:
            kw["accum_out"] = accum.ap
        self.op(self.act, lambda h: h.activation(out.ap, in_.ap, func, **kw),
                self._regs(in_, bias, scale), self._regs(out, accum))

    def tt(self, eng, out, a, b, op):
        self.op(eng, lambda h: h.tensor_tensor(out.ap, a.ap, b.ap, op), self._regs(a, b), self._regs(out))

    def ts(self, eng, out, a, s1, s2, op0, op1=None):
        if op1 is None:
            fn = lambda h: h.tensor_scalar(out.ap, a.ap, self._ap(s1), None, op0)
        else:
            fn = lambda h: h.tensor_scalar(out.ap, a.ap, self._ap(s1), self._ap(s2), op0, op1)
        self.op(eng, fn, self._regs(a, s1, s2), self._regs(out))

    def stt(self, out, a, s, b, op0, op1):
        self.op(self.dve, lambda h: h.scalar_tensor_tensor(out.ap, a.ap, self._ap(s), b.ap, op0, op1),
                self._regs(a, s, b), self._regs(out))

    def cp(self, eng, out, in_):
        if eng is self.act:
            self.op(eng, lambda h: h.copy(out.ap, in_.ap), self._regs(in_), self._regs(out))
        else:
            self.op(eng, lambda h: h.tensor_copy(out.ap, in_.ap), self._regs(in_), self._regs(out))

    def barrier(self):
        engs = [self.pe, self.act, self.dve, self.pool, self.sp]
        for e in engs:
            for f in engs:
                if f is not e and f.cnt > 0:
                    self._wait(e, (f.sem, f.key, f.cnt))
            for i in range(NDMA):
                if self.dexp[i] > 0:
                    self._wait(e, (self.dsem[i], ("d", i), self.dexp[i]))

    def finish(self):
        for tok in self.out_toks:
            self._wait(self.sp, tok)


class Tl:
    def __init__(self, k, stack, name, shape, dtype, nreg=1, psum=False):
        alloc = k.nc.psum_tensor if psum else k.nc.sbuf_tensor
        self.t = stack.enter_context(alloc("t_" + name, list(shape), dtype))
        self.r = [Reg(f"{name}.{i}", ps=psum) for i in range(nreg)]

    def v(self, idx=None, r=0):
        ap = self.t[:] if idx is None else self.t[idx]
        if r is None:
            return V(ap, *self.r)
        if isinstance(r, (list, tuple)):
            return V(ap, *[self.r[i] for i in r])
        return V(ap, self.r[r])


def build_program(S, L, dbg=(), stop_after=None):
    assert S % TT == 0
    NT = S // TT
    NB = S // 128
    nc = bass.Bass("TRN2", target_bir_lowering=False)
    off, rows = img_layout(L)
    NV = L * NVL + NGLOB
    sl = slice(None)

    def din(name, shape, dt=F32):
        return nc.dram_tensor(name, list(shape), dt, kind="ExternalInput").ap()

    x_in = din("x", [S, D])
    pos_in = din("pos", [128, S], I32)
    wsrc = [din(f"wsrc{l}", [rows, CW]) for l in range(L)]
    vecs_in = din("vecs", [128, NV])
    cst_in = din("cst", [128, 3, 128])
    sgb_in = din("sgu_gb", [L, 128, 2, 512])
    bsb_in = din("sgu_bsb", [L, 128, 4, 512])
    sgw_in = din("sgu_w", [L, 128, 4, 128])
    out_d = nc.dram_tensor("out", [S, D], F32, kind="ExternalOutput").ap()

    def scr(name, shape, dt):
        return nc.dram_tensor(name, list(shape), dt,
                              kind="ExternalOutput" if name in dbg else "Internal").ap()

    wall = [scr(f"wall{l}", [rows, CW], BF16) for l in range(L)]
    XT = scr("XT", [DC, 128, S], F32)
    HT = scr("HT", [DC, 128, S], BF16)
    BR = scr("BR", [12, 128, S], BF16)
    QN = scr("QN", [8, 128, S], BF16)
    QR = scr("QR", [4, 128, S], BF16)
    KN = scr("KN", [8, 128, S], BF16)
    KR = scr("KR", [128, S], BF16)
    VV = scr("VV", [NB, 128, 1024], BF16)
    OT = scr("OT", [8, 128, S], BF16)
    CS = scr("CS", [2, 128, S], F32)
    DBG = scr("DBG", [128, 2048], F32)
    DBG2 = scr("DBG2", [8, NB, 128, 132], F32)
    wflat = [w_.rearrange("r c -> (r c)") for w_ in wall]
    wsrc_flat = [w_.rearrange("r c -> (r c)") for w_ in wsrc]

    R_in = Reg("ext_in")
    R_XT = [Reg(f"XT{t}") for t in range(NT)]
    R_HT = [Reg(f"HT{t}") for t in range(NT)]
    R_BR = [Reg(f"BR{t}") for t in range(NT)]
    R_Q = [Reg(f"Q{t}") for t in range(NT)]
    R_K = [Reg(f"K{t}") for t in range(NT)]
    R_OT = [Reg(f"OT{t}") for t in range(8)]
    R_CS = Reg("CS")
    R_w = {}

    with contextlib.ExitStack() as top:
        k = Kb(nc, top)
        pe, act, dve, pool, sp = k.pe, k.act, k.dve, k.pool, k.sp
        vecs = Tl(k, top, "vecs", [128, NV], F32)
        cst = Tl(k, top, "cst", [128, 3, 128], F32)
        ident_b = Tl(k, top, "ident_b", [128, 128], BF16)
        ones_b = Tl(k, top, "ones_b", [128, 128], BF16)
        bnd = Tl(k, top, "bnd", [128, 4], F32)
        PS = [Tl(k, top, f"ps{i}", [128, 512], F32, psum=True) for i in range(6)]
        PB = [Tl(k, top, f"pb{i}", [128, 1024], BF16, psum=True) for i in range(2)]
        k.dma(sp, vecs.v(), V(vecs_in[:, :], R_in))
        k.dma(sp, cst.v(), V(cst_in[:, :, :], R_in))
        k.cp(dve, ident_b.v(), cst.v((sl, 0, sl)))
        k.op(dve, lambda h: h.memset(ones_b.t[:], 1.0), [], ones_b.r)
        _fap = PS[5].t[:, 0:1]
        k._deps(pe, ident_b.r, [], False)
        k.fence = lambda h_: h_.matmul(_fap, ident_b.t[:, :], ident_b.t[:, 0:1], start=True, stop=True)
        ident_f = cst.v((sl, 0, sl))
        maskneg = cst.v((sl, 1, sl))
        triu01 = cst.v((sl, 2, sl))
        rot = [0]

        def ps_next():
            i = rot[0]
            rot[0] = (i + 1) % 3
            return PS[i]

        def vcol(l, name, c):
            j = l * NVL + VEC_OFF[name] + c
            return V(vecs.t[:, j:j + 1], vecs.r[0])

        def gcol(j):
            jj = L * NVL + j
            return V(vecs.t[:, jj:jj + 1], vecs.r[0])

        for l in range(L):
            for name, _ in OBJS:
                o, sz = off[(l, name)]
                npc = max(1, sz // (4 * 1024 * 1024))
                pc = sz // npc
                assert pc * npc == sz and pc % 2048 == 0
                regs = []
                for i in range(npc):
                    a = o + i * pc
                    r = Reg(f"w{l}{name}{i}")
                    k.dma(pool, V(wflat[l][a:a + pc].rearrange("(r c) -> r c", c=2048), r),
                          V(wsrc_flat[l][a:a + pc].rearrange("(r c) -> r c", c=2048), R_in))
                    regs.append(r)
                R_w[(l, name)] = regs

        def wload(dst, l, name, blk0, nblk, per_blk, q=None):
            o, sz = off[(l, name)]
            a = o + blk0 * per_blk
            d = dst.ap
            if nblk == 1:
                src = wflat[l][a:a + per_blk].rearrange("(p x) -> p x", p=128)
                if d.ndim == 3:
                    d = d.rearrange("p a b -> p (a b)")
                elif d.ndim == 4:
                    d = d.rearrange("p a b c -> p (a b c)")
            else:
                src = wflat[l][a:a + nblk * per_blk].rearrange("(g p x) -> p g x", g=nblk, p=128)
                if d.ndim == 4:
                    d = d.rearrange("p g b c -> p g (b c)")
            k.dma(q or sp, V(d, *dst.regs), V(src, *R_w[(l, name)]))

        def rstd_from(ps, dim, rstd):
            k.ts(dve, rstd.v(), ps.v(), 1.0 / dim, EPS, ALU.mult, ALU.add)
            k.actv(rstd.v(), rstd.v(), AF.Sqrt)
            k.op(dve, lambda h: h.reciprocal(rstd.t[:], rstd.t[:]), rstd.r, rstd.r)

        def rmsnorm_fm(src, nch, gfun, dst, sq, rstd):
            ps = ps_next()
            for c in range(nch):
                s = sq[c % 2]
                k.actv(s.v(), src(c), AF.Square)
                k.mm(ps.v(), ones_b.v(), s.v(), c == 0, c == nch - 1)
            rstd_from(ps, nch * 128, rstd)
            for c in range(nch):
                k.stt(dst(c), src(c), gfun(c), rstd.v(), ALU.mult, ALU.mult)

        def gelu(out, src, tmp_a, tmp_b):
            k.actv(tmp_a, src, AF.Square)
            k.ts(dve, tmp_a, tmp_a, 0.044715, 1.0, ALU.mult, ALU.add)
            k.tt(dve, tmp_a, tmp_a, src, ALU.mult)
            k.actv(tmp_b, tmp_a, AF.Sigmoid, scale=1.5957691216057308)
            k.tt(dve, out, tmp_b, src, ALU.mult)

        with contextlib.ExitStack() as st:
            posi = Tl(k, st, "posi", [128, S], I32)
            ang = Tl(k, st, "ang", [128, S], F32)
            t1 = Tl(k, st, "rt1", [128, S], F32)
            t2 = Tl(k, st, "rt2", [128, S], F32)
            ni = Tl(k, st, "rni", [128, S], I32)
            res = Tl(k, st, "rres", [128, 2, S], F32)
            k.dma(sp, posi.v(), V(pos_in[:, :], R_in))
            k.cp(dve, ang.v(), posi.v())
            k.ts(dve, ang.v(), ang.v(), gcol(0), None, ALU.mult)
            for which in range(2):
                src = ang
                if which == 0:
                    k.ts(dve, t2.v(), ang.v(), float(np.pi / 2), None, ALU.add)
                    src = t2
                k.ts(dve, t1.v(), src.v(), float(1.0 / TWO_PI), None, ALU.mult)
                k.cp(dve, ni.v(), t1.v())
                k.cp(dve, t1.v(), ni.v())
                k.stt(t2.v(), t1.v(), -6.28125, src.v(), ALU.mult, ALU.add)
                k.stt(t2.v(), t1.v(), -(TWO_PI - 6.28125), t2.v(), ALU.mult, ALU.add)
                k.ts(dve, t1.v(), t2.v(), 0.0, TWO_PI, ALU.is_lt, ALU.mult)
                k.tt(dve, t2.v(), t2.v(), t1.v(), ALU.add)
                k.ts(dve, t1.v(), t2.v(), TWO_PI, -TWO_PI, ALU.is_ge, ALU.mult)
                k.tt(dve, t2.v(), t2.v(), t1.v(), ALU.add)
                k.ts(dve, t2.v(), t2.v(), -float(np.pi), -3.1415925, ALU.add, ALU.max)
                k.ts(dve, t2.v(), t2.v(), 3.1415925, None, ALU.min)
                k.actv(t1.v(), t2.v(), AF.Sin)
                sgn = gcol(2) if which == 0 else gcol(1)
                k.ts(dve, res.v((sl, which, sl)), t1.v(), sgn, None, ALU.mult)
            k.dma(sp, V(CS.rearrange("w p s -> p w s"), R_CS), res.v())
            k.barrier()

        def phase1(l):
            with contextlib.ExitStack() as st:
                T = lambda name, shape, dt, nreg=1: Tl(k, st, f"p1{l}_{name}", shape, dt, nreg)
                xT = T("xT", [128, 16, 512], F32)
                hT = T("hT", [128, 16, 512], BF16)
                wst = [T(f"w{i}", [128, 2, 16, 128], BF16) for i in range(2)]
                winv = T("winv", [128, 16, 512], BF16)
                poolw = T("poolw", [128, 4, 128], BF16)
                diag = [T(f"diag{i}", [128, 31, 128], BF16, 31) for i in range(2)]
                wsT = T("wsT", [128, 4, 128], BF16)
                sgw = T("sgw", [128, 4, 128], F32)
                sgb = T("sgb", [128, 2, 512], F32)
                bsb = T("bsb", [128, 4, 512], F32)
                sq = [T(f"sq{i}", [128, 512], BF16) for i in range(2)]
                csq = [T(f"csq{i}", [128, 512], BF16) for i in range(2)]
                rstd = T("rstd", [128, 512], F32)
                xtok = T("xtok", [128, 2048], F32) if l == 0 else None
                pa = T("pa", [128, 4, 528], F32)
                pt = [T(f"pt{i}", [128, 528], F32) for i in range(2)]
                invc0 = T("invc0", [128, 4, 16], F32)
                pooled = T("pooled", [128, 4, 512], BF16)
                brout = T("brout", [128, 12, 512], BF16, 3)
                glu = T("glu", [128, 4, 544], BF16)
                sgm = [T(f"sgm{i}", [128, 512], F32) for i in range(2)]
                cy = T("cy", [128, 4, 512], F32)
                lnm = T("lnm", [128, 512], F32)
                lnr = T("lnr", [128, 512], F32)
                lnt = T("lnt", [128, 512], F32)
                ug = T("ug", [128, 4, 512], BF16)
                vg = [T(f"vg{i}", [128, 512], F32) for i in range(2)]
                vst = [T(f"vst{i}", [128, 8], F32) for i in range(2)]
                vnb = T("vnb", [128, 4, 512], BF16, 4)
                ga = [T(f"ga{i}", [128, 512], F32) for i in range(2)]
                lnt2 = [lnt, ga[0]]
                S1, S2 = PS[3], PS[4]

                wload(winv.v(), l, "WIN_V", 0, 1, 128 * 16 * 512)
                o, sz = off[(l, "POOLW")]
                k.dma(sp, poolw.v(), V(wflat[l][o:o + sz].rearrange("(g p d) -> p g d", g=4, p=128), *R_w[(l, "POOLW")]))
                k.dma(sp, sgb.v(), V(sgb_in[l], R_in))
                k.dma(sp, bsb.v(), V(bsb_in[l], R_in))
                k.dma(sp, sgw.v(), V(sgw_in[l], R_in))
                for g in range(4):
                    ps = ps_next()
                    k.tr(V(ps.t[:, 0:128], ps.r[0]), sgw.v((sl, g, sl)), ident_f)
                    k.tt(dve, wsT.v((sl, g, sl)), V(ps.t[:, 0:128], ps.r[0]), triu01, ALU.mult)
                k.op(pool, lambda h: h.memset(invc0.t[:], 0.0), [], invc0.r)
                for g, w in enumerate(POOL_WINDOWS):
                    for t in range(16):
                        val = 1.0 / min(t + 1, w)
                        k.op(pool, lambda h, g=g, t=t, val=val: h.memset(invc0.t[:, g, t:t + 1], val), [], invc0.r)

                for tt in range(NT):
                    t0 = tt * TT
                    if l == 0:
                        for tb in range(4):
                            k.dma(sp, xtok.v(), V(x_in[t0 + tb * 128:t0 + (tb + 1) * 128, :], R_in))
                            for cg in range(4):
                                ps = ps_next()
                                for j in range(4):
                                    c = cg * 4 + j
                                    k.tr(V(ps.t[:, j * 128:(j + 1) * 128], ps.r[0]),
                                         xtok.v((sl, slice(c * 128, (c + 1) * 128))), ident_f, signal=(j == 3))
                                k.cp(act if cg % 2 else dve,
                                     xT.v((sl, slice(cg * 4, cg * 4 + 4), slice(tb * 128, (tb + 1) * 128))),
                                     V(ps.t[:, :].rearrange("p (j t) -> p j t", j=4), ps.r[0]))
                        k.dma(sp, V(XT[:, :, t0:t0 + TT].rearrange("c p s -> p c s"), R_XT[tt]), xT.v())
                    else:
                        k.dma(sp, xT.v(), V(XT[:, :, t0:t0 + TT].rearrange("c p s -> p c s"), R_XT[tt]))
                    rmsnorm_fm(lambda c: xT.v((sl, c, sl)), 16, lambda c: vcol(l, "pre_mix_g", c),
                               lambda c: hT.v((sl, c, sl)), sq, rstd)
                    k.dma(sp, V(HT[:, :, t0:t0 + TT].rearrange("c p s -> p c s"), R_HT[tt]), hT.v())

                    def zchunk(wt, gi, ps):
                        for kc in range(16):
                            k.mm(ps.v(), wt.v((sl, gi, kc, sl)), hT.v((sl, kc, sl)), kc == 0, kc == 15,
                                 signal=(kc == 15))

                    def build_diag(c):
                        dg = diag[c % 2]
                        for kk in range(31):
                            k.ts(dve, dg.v((sl, kk, sl), kk), ident_b.v(), vcol(l, "conv_w", kk * 4 + c), None,
                                 ALU.mult)

                    if tt == 0:
                        k.op(pool, lambda h: h.memset(pa.t[:, :, 0:16], 0.0), [], pa.r)
                    else:
                        k.cp(pool, pa.v((sl, sl, slice(0, 16))), pa.v((sl, sl, slice(512, 528))))
                    for grp in range(2):
                        wt = wst[grp % 2]
                        wload(wt.v(), l, "WIN_A", grp * 2, 2, 128 * 16 * 128)
                        for gi in range(2):
                            ps = ps_next()
                            zchunk(wt, gi, ps)
                            k.cp(act, pa.v((sl, grp * 2 + gi, slice(16, 528))), ps.v())
                    build_diag(0)
                    build_diag(1)
                    for g, w in enumerate(POOL_WINDOWS):
                        cur = lambda lo, hi, g=g: pa.v((sl, g, slice(lo, hi)))
                        sh = 1
                        bi = 0
                        while sh < w:
                            dst = pt[bi % 2]
                            lo = 2 * sh - 1
                            k.tt(dve, V(dst.t[:, lo:528], dst.r[0]), cur(lo, 528), cur(lo - sh, 528 - sh), ALU.add)
                            cur = lambda lo, hi, dst=dst: V(dst.t[:, lo:hi], dst.r[0])
                            sh *= 2
                            bi += 1
                        k.stt(pooled.v((sl, g, sl)), cur(16, 528), 1.0 / w, pa.v((sl, g, slice(16, 528))),
                              ALU.mult, ALU.subtract)
                        if tt == 0:
                            k.tt(dve, V(lnt.t[:, 0:16], lnt.r[0]), cur(16, 32), invc0.v((sl, g, sl)), ALU.mult)
                            k.tt(dve, pooled.v((sl, g, slice(0, 16))), V(lnt.t[:, 0:16], lnt.r[0]),
                                 pa.v((sl, g, slice(16, 32))), ALU.subtract)

                    if tt == 0:
                        k.op(pool, lambda h: h.memset(glu.t[:, :, 0:32], 0.0), [], glu.r)
                    else:
                        k.cp(pool, glu.v((sl, sl, slice(0, 32))), glu.v((sl, sl, slice(512, 544))))
                    for c in range(4):
                        wt = wst[c % 2]
                        wload(wt.v(), l, "WIN_A", 4 + 2 * c, 2, 128 * 16 * 128)
                        psa = ps_next()
                        zchunk(wt, 0, psa)
                        psg = ps_next()
                        zchunk(wt, 1, psg)
                        sg_ = sgm[c % 2]
                        k.actv(sg_.v(), psg.v(), AF.Sigmoid)
                        k.tt(dve, glu.v((sl, c, slice(32, 544))), psa.v(), sg_.v(), ALU.mult)

                    for g in range(4):
                        ps = ps_next()
                        k.mm(ps.v(), poolw.v((sl, g, sl)), pooled.v((sl, g, sl)), True, True)
                        k.actv(brout.v((sl, g, sl), 0), ps.v(), AF.Copy, scale=vcol(l, "pool_scale", g))

                    for grp in range(2):
                        wt = wst[grp % 2]
                        wload(wt.v(), l, "WIN_A", 12 + grp * 2, 2, 128 * 16 * 128)
                        for gi in range(2):
                            ps = ps_next()
                            zchunk(wt, gi, ps)
                            g = grp * 2 + gi
                            gelu(ug.v((sl, g, sl)), ps.v(), ga[0].v(), ga[1].v())

                    for tb in range(4):
                        ps = ps_next()
                        for kc in range(16):
                            k.mm(ps.v(), hT.v((sl, kc, slice(tb * 128, (tb + 1) * 128))), winv.v((sl, kc, sl)),
                                 kc == 0, kc == 15, signal=(kc == 15))
                        vgt = vg[tb % 2]
                        vs_ = vst[tb % 2]
                        gelu(vgt.v(), ps.v(), ga[0].v(), ga[1].v())
                        k.op(dve, lambda h, vs_=vs_, vgt=vgt: h.bn_stats(vs_.t[:, 0:6], vgt.t[:]), vgt.r, vs_.r)
                        k.op(dve, lambda h, vs_=vs_: h.bn_aggr(vs_.t[:, 6:8], vs_.t[:, 0:6]), vs_.r, vs_.r)
                        k.ts(dve, V(vs_.t[:, 7:8], vs_.r[0]), V(vs_.t[:, 7:8], vs_.r[0]), EPS, None, ALU.add)
                        k.actv(V(vs_.t[:, 7:8], vs_.r[0]), V(vs_.t[:, 7:8], vs_.r[0]), AF.Sqrt)
                        k.op(dve, lambda h, vs_=vs_: h.reciprocal(vs_.t[:, 7:8], vs_.t[:, 7:8]), vs_.r, vs_.r)
                        k.ts(dve, vgt.v(), vgt.v(), V(vs_.t[:, 6:7], vs_.r[0]), V(vs_.t[:, 7:8], vs_.r[0]),
                             ALU.subtract, ALU.mult)
                        k.tt(pool, vgt.v(), vgt.v(), sgb.v((sl, 0, sl)), ALU.mult)
                        k.tt(pool, vnb.v((sl, tb, sl), tb), vgt.v(), sgb.v((sl, 1, sl)), ALU.add)

                    for c in range(4):
                        dg = diag[c % 2]
                        ps = ps_next()
                        for kk in range(31):
                            k.mm(ps.v(), dg.v((sl, kk, sl), kk), glu.v((sl, c, slice(kk + 2, kk + 2 + 512))),
                                 kk == 0, kk == 30, signal=(kk == 30))
                        if c + 2 < 4:
                            build_diag(c + 2)
                        k.actv(cy.v((sl, c, sl)), ps.v(), AF.Identity, bias=vcol(l, "conv_b", c))
                        s = sq[c % 2]
                        k.cp(dve, s.v(), cy.v((sl, c, sl)))
                        k.mm(S1.v(), ones_b.v(), s.v(), c == 0, c == 3)
                        s2 = csq[c % 2]
                        k.actv(s2.v(), cy.v((sl, c, sl)), AF.Square)
                        k.mm(S2.v(), ones_b.v(), s2.v(), c == 0, c == 3)

                    for g in range(4):
                        ps = ps_next()
                        for tb in range(4):
                            k.mm(V(ps.t[:, tb * 128:(tb + 1) * 128], ps.r[0]),
                                 vnb.v((sl, tb, slice(g * 128, (g + 1) * 128)), tb), wsT.v((sl, g, sl)), True, True,
                                 signal=(tb == 3))
                        k.tt(dve, lnt.v(), ps.v(), bsb.v((sl, g, sl)), ALU.add)
                        k.tt(dve, brout.v((sl, 8 + g, sl), 2), lnt.v(), ug.v((sl, g, sl)), ALU.mult)

                    k.ts(dve, lnm.v(), S1.v(), 1.0 / 512, None, ALU.mult)
                    k.tt(dve, lnt.v(), lnm.v(), lnm.v(), ALU.mult)
                    k.stt(lnr.v(), S2.v(), 1.0 / 512, lnt.v(), ALU.mult, ALU.subtract)
                    k.ts(dve, lnr.v(), lnr.v(), EPS, None, ALU.add)
                    k.actv(lnr.v(), lnr.v(), AF.Sqrt)
                    k.op(dve, lambda h: h.reciprocal(lnr.t[:], lnr.t[:]), lnr.r, lnr.r)
                    for c in range(4):
                        lt = lnt2[c % 2]
                        k.tt(dve, lt.v(), cy.v((sl, c, sl)), lnm.v(), ALU.subtract)
                        k.tt(dve, lt.v(), lt.v(), lnr.v(), ALU.mult)
                        k.actv(brout.v((sl, 4 + c, sl), 1), lt.v(), AF.Silu,
                               scale=vcol(l, "conv_norm_g", c), bias=vcol(l, "conv_norm_b", c))
                    k.dma(sp, V(BR[:, :, t0:t0 + TT].rearrange("c p s -> p c s"), R_BR[tt]), brout.v(None, None))

        def phase1c(l):
            with contextlib.ExitStack() as st:
                T = lambda name, shape, dt, nreg=1: Tl(k, st, f"p1c{l}_{name}", shape, dt, nreg)
                hT = T("hT", [128, 16, 512], BF16)
                wst = [T(f"w{i}", [128, 2, 16, 128], BF16) for i in range(2)]
                wuq = T("wuq", [128, 16, 4, 128], BF16)
                wkk = T("wkk", [128, 8, 4, 128], BF16)
                wkv = T("wkv", [128, 4, 1024], BF16)
                cs = T("cs", [128, 2, 512], F32)
                cq = T("cq", [128, 4, 512], F32)
                cqn = T("cqn", [128, 4, 512], BF16)
                ckv = T("ckv", [128, 4, 512], F32)
                ckvn = T("ckvn", [128, 4, 512], BF16)
                sq = [T(f"sq{i}", [128, 512], BF16) for i in range(2)]
                rstd = T("rstd", [128, 512], F32)
                r1 = T("r1", [128, 512], F32)
                r2 = T("r2", [128, 512], F32)
                qn = T("qn", [128, 8, 512], BF16)
                qr = T("qr", [128, 4, 512], BF16)
                kn = T("kn", [128, 8, 512], BF16)
                krt = T("krt", [128, 512], BF16)
                vt = T("vt", [128, 4, 1024], BF16)
                sqn = T("sqn", [128, 8, 512], BF16)
                sqr = T("sqr", [128, 4, 512], BF16)
                mx = T("mx", [128, 2], F32)
                hones = [T(f"hones{i}", [128, 128], BF16) for i in range(2)]
                for i in range(2):
                    k.op(pool, lambda h_, i=i: h_.memset(hones[i].t[:], 0.0), [], hones[i].r)
                    k.op(pool, lambda h_, i=i: h_.memset(hones[i].t[64 * i:64 * i + 64, :], 1.0), [], hones[i].r)

                wload(wuq.v(), l, "WUQ", 0, 16, 128 * 4 * 128)
                wload(wkk.v(), l, "WUKV_K", 0, 8, 128 * 4 * 128)
                wload(wkv.v(), l, "WUKV_V", 0, 1, 128 * 4 * 1024)
                k.op(dve, lambda h: h.memset(bnd.t[:], 0.0), [], bnd.r)

                def rope(psA, psB, dst):
                    k.tt(dve, r1.v(), psA.v(), cs.v((sl, 0, sl)), ALU.mult)
                    k.tt(dve, r2.v(), psB.v(), cs.v((sl, 1, sl)), ALU.mult)
                    k.tt(pool, dst, r1.v(), r2.v(), ALU.add)

                def norm_max(col, parts):
                    ps = ps_next()
                    for i, (v, lt) in enumerate(parts):
                        k.mm(ps.v(), lt, v, i == 0, i == len(parts) - 1)
                    k.op(dve, lambda h: h.reduce_max(mx.t[:, 0:1], ps.t[:], AX.X), ps.r, mx.r)
                    k.tt(dve, V(bnd.t[:, col:col + 1], bnd.r[0]), V(bnd.t[:, col:col + 1], bnd.r[0]),
                         V(mx.t[:, 0:1], mx.r[0]), ALU.max)

                for tt in range(NT):
                    t0 = tt * TT
                    k.dma(sp, hT.v(), V(HT[:, :, t0:t0 + TT].rearrange("c p s -> p c s"), R_HT[tt]))
                    k.dma(sp, cs.v(), V(CS[:, :, t0:t0 + TT].rearrange("w p s -> p w s"), R_CS))

                    def zchunk(wt, gi, ps):
                        for kc in range(16):
                            k.mm(ps.v(), wt.v((sl, gi, kc, sl)), hT.v((sl, kc, sl)), kc == 0, kc == 15,
                                 signal=(kc == 15))

                    for grp in range(5):
                        wt = wst[grp % 2]
                        wload(wt.v(), l, "WIN_C", grp * 2, 2, 128 * 16 * 128)
                        if grp < 4:
                            dst = cq if grp < 2 else ckv
                            for gi in range(2):
                                ps = ps_next()
                                zchunk(wt, gi, ps)
                                k.cp(act, dst.v((sl, (grp % 2) * 2 + gi, sl)), ps.v())
                        else:
                            psA = ps_next()
                            zchunk(wt, 0, psA)
                            psB = ps_next()
                            zchunk(wt, 1, psB)
                            rope(psA, psB, krt.v())
                    rmsnorm_fm(lambda c: cq.v((sl, c, sl)), 4, lambda c: vcol(l, "q_norm_g", c),
                               lambda c: cqn.v((sl, c, sl)), sq, rstd)
                    rmsnorm_fm(lambda c: ckv.v((sl, c, sl)), 4, lambda c: vcol(l, "kv_norm_g", c),
                               lambda c: ckvn.v((sl, c, sl)), sq, rstd)

                    def small(wtile, ch, src, ps):
                        for kc in range(4):
                            k.mm(ps.v(), wtile.v((sl, ch, kc, sl)), src.v((sl, kc, sl)), kc == 0, kc == 3,
                                 signal=(kc == 3))

                    for h in range(8):
                        ps = ps_next()
                        small(wuq, h, cqn, ps)
                        k.cp(act, qn.v((sl, h, sl)), ps.v())
                    for j in range(4):
                        psA = ps_next()
                        small(wuq, 8 + j, cqn, psA)
                        psB = ps_next()
                        small(wuq, 12 + j, cqn, psB)
                        rope(psA, psB, qr.v((sl, j, sl)))
                    for h in range(8):
                        ps = ps_next()
                        small(wkk, h, ckvn, ps)
                        k.cp(act, kn.v((sl, h, sl)), ps.v())
                    for tb in range(4):
                        for half in range(2):
                            ps = ps_next()
                            for kc in range(4):
                                k.mm(ps.v(), ckvn.v((sl, kc, slice(tb * 128, (tb + 1) * 128))),
                                     wkv.v((sl, kc, slice(half * 512, (half + 1) * 512))), kc == 0, kc == 3,
                                     signal=(kc == 3))
                            k.cp(dve if half else act, vt.v((sl, tb, slice(half * 512, (half + 1) * 512))), ps.v())
                    k.actv(sqn.v(), qn.v(), AF.Square)
                    k.actv(sqr.v(), qr.v(), AF.Square)
                    for h in range(8):
                        norm_max(0, [(sqn.v((sl, h, sl)), ones_b.v()), (sqr.v((sl, h // 2, sl)), hones[h % 2].v())])
                    k.actv(sqn.v(), kn.v(), AF.Square)
                    k.actv(sq[0].v(), krt.v(), AF.Square)
                    for h in range(8):
                        norm_max(1, [(sqn.v((sl, h, sl)), ones_b.v()), (sq[0].v(), hones[0].v())])
                    k.dma(sp, V(QN[:, :, t0:t0 + TT].rearrange("c p s -> p c s"), R_Q[tt]), qn.v())
                    k.dma(sp, V(QR[:, :, t0:t0 + TT].rearrange("c p s -> p c s"), R_Q[tt]), qr.v())
                    k.dma(sp, V(KN[:, :, t0:t0 + TT].rearrange("c p s -> p c s"), R_K[tt]), kn.v())
                    k.dma(sp, V(KR[:, t0:t0 + TT], R_K[tt]), krt.v())
                    k.dma(sp, V(VV[tt * 4:(tt + 1) * 4, :, :].rearrange("b p c -> p b c"), R_K[tt]), vt.v())
                k.tt(dve, V(bnd.t[:, 3:4], bnd.r[0]), V(bnd.t[:, 0:1], bnd.r[0]), V(bnd.t[:, 1:2], bnd.r[0]), ALU.mult)
                k.actv(V(bnd.t[:, 3:4], bnd.r[0]), V(bnd.t[:, 3:4], bnd.r[0]), AF.Sqrt)
                k.ts(dve, V(bnd.t[:, 2:3], bnd.r[0]), V(bnd.t[:, 3:4], bnd.r[0]), -SCALE, None, ALU.mult)

        def phase2(l):
            with contextlib.ExitStack() as st:
                T = lambda name, shape, dt, nreg=1: Tl(k, st, f"p2{l}_{name}", shape, dt, nreg)
                KRz = [T(f"KRz{i}", [128, S], BF16) for i in range(2)]
                KTh = [T(f"KT{i}", [128, S], BF16) for i in range(2)]
                Vh = [T(f"V{i}", [128, NB, 128], BF16) for i in range(2)]
                QNh = [T(f"QN{i}", [128, S], BF16) for i in range(2)]
                QRp = [T(f"QR{i}", [128, S], BF16) for i in range(2)]
                OTh = [T(f"OT{i}", [128, S], BF16) for i in range(2)]
                pbuf = [T(f"p{i}", [128, 512], BF16) for i in range(3)]
                pT = [T(f"pT{i}", [128, 512], BF16) for i in range(3)]
                sd = [T(f"sd{i}", [128, 128], F32) for i in range(4)]
                rs = [T(f"rs{i}", [128, 16], F32) for i in range(4)]
                rsum = [T(f"rsum{i}", [128, 2], F32) for i in range(4)]
                ob = [T(f"ob{i}", [128, 128], BF16) for i in range(4)]
                negb = V(bnd.t[:, 2:3], bnd.r[0])
                allK = R_K
                dbg3 = [T(f"dbg3{i}", [128, 132], F32) for i in range(2)] if "DBG2" in dbg else None
                for i in range(2):
                    k.dma(sp, KRz[i].v(), V(KR[:, :], *allK))
                    z0 = 64 * (1 - i)
                    k.op(pool, lambda h_, i=i, z0=z0: h_.memset(KRz[i].t[z0:z0 + 64, :], 0.0), [], KRz[i].r)
                gc = [0]
                for h in range(8):
                    b = h % 2
                    k.dma(sp, KTh[b].v(), V(KN[h], *allK))
                    for v0 in range(0, NB, 8):
                        v1 = min(NB, v0 + 8)
                        k.dma(sp, Vh[b].v((sl, slice(v0, v1), sl)),
                              V(VV[v0:v1, :, h * 128:(h + 1) * 128].rearrange("b p c -> p b c"), *allK))
                    k.dma(sp, QNh[b].v(), V(QN[h], *R_Q))
                    k.dma(sp, QRp[b].v(), V(QR[h // 2], *R_Q))
                    chunks = []
                    for qb in range(NB):
                        nk = qb + 1
                        nch = (nk + 3) // 4
                        for j in range(nch):
                            kb0 = j * 4
                            nb = min(4, nk - kb0)
                            chunks.append((qb, j, kb0, nb, nb * 128, j == nch - 1, gc[0]))
                            gc[0] += 1
                    n = len(chunks)

                    def stage_ab(c):
                        qb, j, kb0, nb, ncols, last, g = c
                        q0 = qb * 128
                        Sb = PS[g % 3]
                        rsb = rs[qb % 4]
                        if j == 0:
                            k.op(pool, lambda h_, rsb=rsb: h_.memset(rsb.t[:], 0.0), [], rsb.r)
                        Sv = lambda lo, hi: V(Sb.t[:, lo:hi], Sb.r[0])
                        k.mm(Sv(0, ncols), QNh[b].v((sl, slice(q0, q0 + 128))),
                             KTh[b].v((sl, slice(kb0 * 128, kb0 * 128 + ncols))), True, False, signal=False)
                        k.mm(Sv(0, ncols), QRp[b].v((sl, slice(q0, q0 + 128))),
                             KRz[h % 2].v((sl, slice(kb0 * 128, kb0 * 128 + ncols))), False, True)
                        pb = pbuf[g % 3]
                        nd = ncols - 128 if last else ncols
                        if last:
                            sdb = sd[qb % 4]
                            k.tt(dve, sdb.v(), Sv(nd, ncols), maskneg, ALU.add)
                        if nd > 0:
                            k.actv(V(pb.t[:, 0:nd], pb.r[0]), Sv(0, nd), AF.Exp, bias=negb, scale=SCALE,
                                   accum=V(rsb.t[:, 2 * j:2 * j + 1], rsb.r[0]))
                        if last:
                            k.actv(V(pb.t[:, nd:ncols], pb.r[0]), sdb.v(), AF.Exp, bias=negb, scale=SCALE,
                                   accum=V(rsb.t[:, 2 * j + 1:2 * j + 2], rsb.r[0]))

                    def stage_c(c):
                        qb, j, kb0, nb, ncols, last, g = c
                        pb = pbuf[g % 3]
                        PTp = PB[g % 2]
                        ptb = pT[g % 3]
                        for i in range(nb):
                            k.tr(V(PTp.t[:, i * 128:(i + 1) * 128], PTp.r[0]),
                                 V(pb.t[:, i * 128:(i + 1) * 128], pb.r[0]), ident_b.v(), signal=(i == nb - 1))
                        k.cp(dve, V(ptb.t[:, 0:ncols], ptb.r[0]), V(PTp.t[:, 0:ncols], PTp.r[0]))

                    def stage_d(c):
                        qb, j, kb0, nb, ncols, last, g = c
                        q0 = qb * 128
                        O = PS[3 + (qb % 2)]
                        ptb = pT[g % 3]
                        for i in range(nb):
                            k.mm(V(O.t[:, 0:128], O.r[0]), V(ptb.t[:, i * 128:(i + 1) * 128], ptb.r[0]),
                                 Vh[b].v((sl, kb0 + i, sl)), (j == 0 and i == 0), (last and i == nb - 1),
                                 signal=(i == nb - 1))
                        if not last:
                            return
                        rsb = rs[qb % 4]
                        rsm = rsum[qb % 4]
                        k.op(dve, lambda h_, rsm=rsm, rsb=rsb: h_.reduce_sum(rsm.t[:, 0:1], rsb.t[:], AX.X),
                             rsb.r, rsm.r)
                        k.op(dve, lambda h_, rsm=rsm: h_.reciprocal(rsm.t[:, 1:2], rsm.t[:, 0:1]), rsm.r, rsm.r)
                        obb = ob[qb % 4]
                        k.actv(obb.v(), V(O.t[:, 0:128], O.r[0]), AF.Copy, scale=V(rsm.t[:, 1:2], rsm.r[0]))
                        PTp = PB[g % 2]
                        k.tr(V(PTp.t[:, 512:640], PTp.r[0]), obb.v(), ident_b.v())
                        k.cp(dve, OTh[b].v((sl, slice(q0, q0 + 128))), V(PTp.t[:, 512:640], PTp.r[0]))

                    for s in range(n + 2):
                        if s < n:
                            stage_ab(chunks[s])
                        if 1 <= s <= n:
                            stage_c(chunks[s - 1])
                        if 2 <= s <= n + 1:
                            stage_d(chunks[s - 2])
                    k.dma(sp, V(OT[h], R_OT[h]), OTh[b].v())

        def phase3a(l):
            with contextlib.ExitStack() as st:
                T = lambda name, shape, dt, nreg=1: Tl(k, st, f"p3a{l}_{name}", shape, dt, nreg)
                xT = T("xT", [128, 16, 512], F32)
                hT = T("hT", [128, 16, 512], BF16)
                br = T("br", [128, 12, 512], BF16)
                ot = T("ot", [128, 8, 512], BF16)
                merged = T("merged", [128, 16, 512], BF16, 16)
                fT = T("fT", [128, 16, 512], F32, 16)
                gw = [T(f"gw{i}", [128, 4, 16, 128], BF16) for i in range(2)]
                pw = [T(f"pw{i}", [128, 20, 128], BF16) for i in range(2)]
                sgm = [T(f"sgm{i}", [128, 512], F32) for i in range(2)]
                acc = T("acc", [128, 512], F32)
                tmp = [T(f"tmp{i}", [128, 512], F32) for i in range(2)]
                sq = [T(f"sq{i}", [128, 512], BF16) for i in range(2)]
                rstd = T("rstd", [128, 512], F32)
                SS = PS[3]
                kcs = [(0, 4, br, 0), (4, 4, br, 4), (8, 4, br, 8), (12, 8, ot, 0)]
                for tt in range(NT):
                    t0 = tt * TT
                    k.dma(sp, xT.v(), V(XT[:, :, t0:t0 + TT].rearrange("c p s -> p c s"), R_XT[tt]))
                    k.dma(sp, hT.v(), V(HT[:, :, t0:t0 + TT].rearrange("c p s -> p c s"), R_HT[tt]))
                    k.dma(sp, br.v(), V(BR[:, :, t0:t0 + TT].rearrange("c p s -> p c s"), R_BR[tt]))
                    k.dma(sp, ot.v(), V(OT[:, :, t0:t0 + TT].rearrange("c p s -> p c s"), *R_OT))
                    for m in range(16):
                        g_ = gw[m % 2]
                        p_ = pw[m % 2]
                        wload(g_.v(), l, "GATE", m, 1, 128 * 64 * 128)
                        wload(p_.v(), l, "PROJ", m, 1, 128 * 20 * 128)
                        for b in range(4):
                            G = ps_next()
                            for kc in range(16):
                                k.mm(G.v(), g_.v((sl, b, kc, sl)), hT.v((sl, kc, sl)), kc == 0, kc == 15,
                                     signal=(kc == 15))
                            Y = ps_next()
                            k0, nkc, src, s0 = kcs[b]
                            for kc in range(nkc):
                                k.mm(Y.v(), p_.v((sl, k0 + kc, sl)), src.v((sl, s0 + kc, sl)), kc == 0,
                                     kc == nkc - 1, signal=(kc == nkc - 1))
                            s_ = sgm[b % 2]
                            k.actv(s_.v(), G.v(), AF.Sigmoid)
                            if b == 0:
                                k.tt(dve, acc.v(), Y.v(), s_.v(), ALU.mult)
                            else:
                                t_ = tmp[b % 2]
                                k.tt(dve, t_.v(), Y.v(), s_.v(), ALU.mult)
                                if b < 3:
                                    k.tt(pool, acc.v(), acc.v(), t_.v(), ALU.add)
                                else:
                                    k.tt(pool, merged.v((sl, m, sl), m), acc.v(), t_.v(), ALU.add)
                    for mg in range(4):
                        wo = gw[mg % 2]
                        wload(wo.v(), l, "WOUT", mg * 4, 4, 128 * 16 * 128)
                        for mi in range(4):
                            m2 = mg * 4 + mi
                            Fb = ps_next()
                            for kc in range(16):
                                k.mm(Fb.v(), wo.v((sl, mi, kc, sl)), merged.v((sl, kc, sl), kc), kc == 0, kc == 15,
                                     signal=(kc == 15))
                            k.cp(dve, fT.v((sl, m2, sl), m2), Fb.v())
                            s = sq[m2 % 2]
                            k.actv(s.v(), fT.v((sl, m2, sl), m2), AF.Square)
                            k.mm(SS.v(), ones_b.v(), s.v(), m2 == 0, m2 == 15)
                    rstd_from(SS, 2048, rstd)
                    for c in range(16):
                        t_ = tmp[c % 2]
                        k.stt(t_.v(), fT.v((sl, c, sl), c), vcol(l, "post_mix_g", c), rstd.v(), ALU.mult, ALU.mult)
                        k.tt(dve if c % 2 else pool, xT.v((sl, c, sl)), xT.v((sl, c, sl)), t_.v(), ALU.add)
                    k.dma(sp, V(XT[:, :, t0:t0 + TT].rearrange("c p s -> p c s"), R_XT[tt]), xT.v())

        def phase3b(l):
            with contextlib.ExitStack() as st:
                T = lambda name, shape, dt, nreg=1: Tl(k, st, f"p3b{l}_{name}", shape, dt, nreg)
                xT = T("xT", [128, 16, 512], F32)
                h2 = T("h2", [128, 16, 512], BF16)
                aT = T("aT", [128, 64, 512], BF16, 64)
                fT = T("fT", [128, 16, 512], F32, 16)
                wb = [T(f"w{i}", [128, 4, 16, 128], BF16) for i in range(2)]
                rr = [T(f"rr{i}", [128, 512], BF16) for i in range(2)]
                tmp = [T(f"tmp{i}", [128, 512], F32) for i in range(2)]
                sq = [T(f"sq{i}", [128, 512], BF16) for i in range(2)]
                rstd = T("rstd", [128, 512], F32)
                SS = PS[3]
                for tt in range(NT):
                    t0 = tt * TT
                    k.dma(sp, xT.v(), V(XT[:, :, t0:t0 + TT].rearrange("c p s -> p c s"), R_XT[tt]))
                    rmsnorm_fm(lambda c: xT.v((sl, c, sl)), 16, lambda c: vcol(l, "pre_mlp_g", c),
                               lambda c: h2.v((sl, c, sl)), sq, rstd)
                    for mg in range(16):
                        wu = wb[mg % 2]
                        wload(wu.v(), l, "WUP", mg * 4, 4, 128 * 16 * 128)
                        for mi in range(4):
                            m = mg * 4 + mi
                            ps = ps_next()
                            for kc in range(16):
                                k.mm(ps.v(), wu.v((sl, mi, kc, sl)), h2.v((sl, kc, sl)), kc == 0, kc == 15,
                                     signal=(kc == 15))
                            r_ = rr[m % 2]
                            k.actv(r_.v(), ps.v(), AF.Relu)
                            k.tt(pool if m % 2 else dve, aT.v((sl, m, sl), m), r_.v(), r_.v(), ALU.mult)
                    for m2 in range(16):
                        wd = wb[m2 % 2]
                        wload(wd.v(), l, "WDOWN", m2, 1, 128 * 64 * 128)
                        wdv = wd.t[:].rearrange("p a b c -> p (a b) c")
                        Fb = ps_next()
                        for kc in range(64):
                            k.mm(Fb.v(), V(wdv[:, kc, :], wd.r[0]), aT.v((sl, kc, sl), kc), kc == 0, kc == 63,
                                 signal=(kc == 63))
                        k.cp(dve, fT.v((sl, m2, sl), m2), Fb.v())
                        s = sq[m2 % 2]
                        k.actv(s.v(), fT.v((sl, m2, sl), m2), AF.Square)
                        k.mm(SS.v(), ones_b.v(), s.v(), m2 == 0, m2 == 15)
                    rstd_from(SS, 2048, rstd)
                    for c in range(16):
                        t_ = tmp[c % 2]
                        k.stt(t_.v(), fT.v((sl, c, sl), c), vcol(l, "post_mlp_g", c), rstd.v(), ALU.mult, ALU.mult)
                        k.tt(dve if c % 2 else pool, xT.v((sl, c, sl)), xT.v((sl, c, sl)), t_.v(), ALU.add)
                    if l < L - 1:
                        k.dma(sp, V(XT[:, :, t0:t0 + TT].rearrange("c p s -> p c s"), R_XT[tt]), xT.v())
                    else:
                        for tb in range(4):
                            c0 = 4 * (tb % 2)
                            regs = list(range(c0, c0 + 4))
                            stg = fT.t[:, c0:c0 + 4, :].rearrange("p a b -> p (a b)")
                            for cg in range(4):
                                ps = ps_next()
                                for j in range(4):
                                    c = cg * 4 + j
                                    k.tr(V(ps.t[:, j * 128:(j + 1) * 128], ps.r[0]),
                                         xT.v((sl, c, slice(tb * 128, (tb + 1) * 128))), ident_f, signal=(j == 3))
                                k.cp(act if cg % 2 else dve,
                                     V(stg[:, cg * 512:(cg + 1) * 512], *[fT.r[i] for i in regs]), ps.v())
                            k.dma(sp, V(out_d[t0 + tb * 128:t0 + (tb + 1) * 128, :], Reg("o")),
                                  V(stg, *[fT.r[i] for i in regs]), is_out=True)

        phases = []
        for l in range(L):
            phases += [("p1", phase1, l), ("p1c", phase1c, l), ("p2", phase2, l), ("p3a", phase3a, l),
                       ("p3b", phase3b, l)]
        for name, fn, l in phases:
            fn(l)
            k.barrier()
            if stop_after == (name, l):
                break
        if stop_after is not None:
            for i in range(NDMA):
                if k.dexp[i] > 0:
                    k._wait(sp, (k.dsem[i], ("d", i), k.dexp[i]))
        k.finish()
    return nc


_CACHE = {}


def make_inputs(inp, S, L, n_cores):
    img = build_image(inp, L)
    vecs = build_vecs(inp, L)
    cst = build_consts()
    sgb = np.zeros((L, 128, 2, 512), np.float32)
    bsb = np.zeros((L, 128, 4, 512), np.float32)
    sgw = np.zeros((L, 128, 4, 128), np.float32)
    for l in range(L):
        sgb[l, :, 0, :] = np.asarray(inp["sgu_norm_g"][l])[None, :]
        sgb[l, :, 1, :] = np.asarray(inp["sgu_norm_b"][l])[None, :]
        b = np.asarray(inp["sgu_b"][l])
        bsb[l] = np.tile(b[None, :, :], (128, 1, 4))
        sgw[l] = np.asarray(inp["sgu_w"][l]).transpose(1, 0, 2)
    x = np.asarray(inp["x"], np.float32)
    pos = np.asarray(inp["positions"]).astype(np.int32)
    maps = []
    for c in range(n_cores):
        maps.append({
            "x": np.ascontiguousarray(x[c, :S]),
            "pos": np.ascontiguousarray(np.broadcast_to(pos[c, :S][None, :], (128, S))),
            **{f"wsrc{l}": img[l] for l in range(L)}, "vecs": vecs, "cst": cst, "sgu_gb": sgb, "sgu_bsb": bsb, "sgu_w": sgw,
        })
    return maps


def kernel(**inputs):
    S, L, n = 4096, 2, 8
    key = (S, L)
    if key not in _CACHE:
        _CACHE[key] = build_program(S, L)
    nc = _CACHE[key]
    maps = make_inputs(inputs, S, L, n)
    res = run_bass_kernel_spmd(nc, maps, core_ids=list(range(n)))
    out = np.stack([np.asarray(res.results[c]["out"], np.float32) for c in range(n)], axis=0)
    return out
```
